# Optimizing a Trainium2 kernel written in Bass

```python
import jax, jax.numpy as jnp
from jax import lax
import numpy as np

D_MODEL = 4096
BATCH = 4
SEQ = 2048
DEPTH = 1

GRID_W = 64
ROPE_THETA = 10000.0
Q_BLOCK = 128
EPS = 1e-6

MLA_HEADS = 16
MLA_NOPE = 128
MLA_ROPE = 64
MLA_V = 128
Q_LORA = 1024
KV_LORA = 512
MLA_SCALE = (MLA_NOPE + MLA_ROPE) ** -0.5

GQA_Q_HEADS = 16
GQA_KV_HEADS = 4
GQA_HEAD_DIM = 128
GQA_SCALE = GQA_HEAD_DIM ** -0.5

N_BRANCH = 2
IN_SIZES = (Q_LORA, KV_LORA, MLA_ROPE, GQA_Q_HEADS * GQA_HEAD_DIM,
            GQA_KV_HEADS * GQA_HEAD_DIM, GQA_KV_HEADS * GQA_HEAD_DIM, N_BRANCH * D_MODEL)
IN_WIDTH = Q_LORA + KV_LORA + MLA_ROPE + (GQA_Q_HEADS + 2 * GQA_KV_HEADS) * GQA_HEAD_DIM + N_BRANCH * D_MODEL

D_FF = ((8 * D_MODEL + 3 * 256 - 1) // (3 * 256)) * 256

kernel_name = "hybrid_gated_mla_gqa_axial_encoder"


def rmsnorm(x, g):
    xf = x.astype(jnp.float32)
    y = xf * lax.rsqrt(jnp.mean(xf * xf, axis=-1, keepdims=True) + EPS)
    return (y * g.astype(jnp.float32)).astype(x.dtype)


def rope_table(pos, dim):
    inv = ROPE_THETA ** (-jnp.arange(0, dim, 2, dtype=jnp.float32) / dim)
    ang = pos.astype(jnp.float32)[:, None] * inv[None, :]
    ang = jnp.concatenate([ang, ang], axis=-1)
    return jnp.cos(ang), jnp.sin(ang)


def apply_rope(x, pos):
    d = x.shape[-1]
    cos, sin = rope_table(pos, d)
    x1, x2 = x[..., : d // 2], x[..., d // 2:]
    rot = jnp.concatenate([-x2, x1], axis=-1)
    y = x.astype(jnp.float32) * cos[None, :, None, :] + rot.astype(jnp.float32) * sin[None, :, None, :]
    return y.astype(x.dtype)


def axial_rope(x, row_idx, col_idx):
    half = x.shape[-1] // 2
    return jnp.concatenate([apply_rope(x[..., :half], row_idx),
                            apply_rope(x[..., half:], col_idx)], axis=-1)


def block_attention(q, k, v, scale):
    B, S, H, Dk = q.shape
    Hk = k.shape[2]
    G = H // Hk
    Dv = v.shape[-1]
    nb = S // Q_BLOCK
    qb = q.reshape(B, nb, Q_BLOCK, Hk, G, Dk).transpose(1, 0, 2, 3, 4, 5)

    def one_block(q_blk):
        s = jnp.einsum("bqhgd,bkhd->bhgqk", q_blk, k,
                       preferred_element_type=jnp.float32) * scale
        p = jax.nn.softmax(s, axis=-1).astype(v.dtype)
        return jnp.einsum("bhgqk,bkhd->bqhgd", p, v)

    o = lax.map(one_block, qb)
    return o.transpose(1, 0, 2, 3, 4, 5).reshape(B, S, H, Dv)


def setup_inputs(seed: int = 0) -> dict:
    key = jax.random.key(seed)
    ks = iter(jax.random.split(key, 24))

    def dense(fan_in, fan_out):
        return jax.random.normal(next(ks), (DEPTH, fan_in, fan_out), jnp.float32) * fan_in ** -0.5

    def gain(n):
        return 1.0 + 0.01 * jax.random.normal(next(ks), (DEPTH, n), jnp.float32)

    x = jax.random.normal(next(ks), (BATCH, SEQ, D_MODEL), jnp.float32)
    inputs = {}
    inputs["x"] = x
    inputs["g_attn"] = gain(D_MODEL)
    inputs["w_in"] = dense(D_MODEL, IN_WIDTH)
    inputs["g_q_a"] = gain(Q_LORA)
    inputs["w_q_b"] = dense(Q_LORA, MLA_HEADS * (MLA_NOPE + MLA_ROPE))
    inputs["g_kv_a"] = gain(KV_LORA)
    inputs["w_kv_b"] = dense(KV_LORA, MLA_HEADS * (MLA_NOPE + MLA_V))
    inputs["g_qn"] = gain(GQA_HEAD_DIM)
    inputs["g_kn"] = gain(GQA_HEAD_DIM)
    inputs["w_branch_a"] = dense(MLA_HEADS * MLA_V, D_MODEL)
    inputs["w_branch_b"] = dense(GQA_Q_HEADS * GQA_HEAD_DIM, D_MODEL)
    inputs["w_o"] = dense(D_MODEL, D_MODEL)
    inputs["g_ffn"] = gain(D_MODEL)
    inputs["w_gate"] = dense(D_MODEL, D_FF)
    inputs["w_up"] = dense(D_MODEL, D_FF)
    inputs["w_down"] = dense(D_FF, D_MODEL)
    inputs["g_final"] = 1.0 + 0.01 * jax.random.normal(next(ks), (D_MODEL,), jnp.float32)
    return inputs


def reference(x, g_attn, w_in, g_q_a, w_q_b, g_kv_a, w_kv_b, g_qn, g_kn,
              w_branch_a, w_branch_b, w_o, g_ffn, w_gate, w_up, w_down, g_final):
    B, S, _ = x.shape
    rows = S // GRID_W
    row_idx = jnp.repeat(jnp.arange(rows, dtype=jnp.int32), GRID_W)
    col_idx = jnp.tile(jnp.arange(GRID_W, dtype=jnp.int32), rows)
    split_at = [int(v) for v in np.cumsum(IN_SIZES)[:-1]]

    for l in range(DEPTH):
        h = rmsnorm(x, g_attn[l])
        z = h @ w_in[l]
        z_qa, z_kva, z_kpe, z_q, z_k, z_v, z_gate = jnp.split(z, split_at, axis=-1)

        c_q = rmsnorm(z_qa, g_q_a[l])
        q_a = (c_q @ w_q_b[l]).reshape(B, S, MLA_HEADS, MLA_NOPE + MLA_ROPE)
        q_a = jnp.concatenate([q_a[..., :MLA_NOPE],
                               axial_rope(q_a[..., MLA_NOPE:], row_idx, col_idx)], axis=-1)
        c_kv = rmsnorm(z_kva, g_kv_a[l])
        kv = (c_kv @ w_kv_b[l]).reshape(B, S, MLA_HEADS, MLA_NOPE + MLA_V)
        k_nope, v_a = kv[..., :MLA_NOPE], kv[..., MLA_NOPE:]
        k_pe = axial_rope(z_kpe[:, :, None, :], row_idx, col_idx)
        k_a = jnp.concatenate([k_nope, jnp.broadcast_to(k_pe, (B, S, MLA_HEADS, MLA_ROPE))], axis=-1)
        o_a = block_attention(q_a, k_a, v_a, MLA_SCALE).reshape(B, S, MLA_HEADS * MLA_V)

        q_b = rmsnorm(z_q.reshape(B, S, GQA_Q_HEADS, GQA_HEAD_DIM), g_qn[l])
        k_b = rmsnorm(z_k.reshape(B, S, GQA_KV_HEADS, GQA_HEAD_DIM), g_kn[l])
        v_b = z_v.reshape(B, S, GQA_KV_HEADS, GQA_HEAD_DIM)
        q_b = axial_rope(q_b, row_idx, col_idx)
        k_b = axial_rope(k_b, row_idx, col_idx)
        o_b = block_attention(q_b, k_b, v_b, GQA_SCALE).reshape(B, S, GQA_Q_HEADS * GQA_HEAD_DIM)

        gates = jax.nn.sigmoid(z_gate)
        g_a, g_b = gates[..., :D_MODEL], gates[..., D_MODEL:]
        m = g_a * (o_a @ w_branch_a[l]) + g_b * (o_b @ w_branch_b[l])
        x = x + m @ w_o[l]

        h2 = rmsnorm(x, g_ffn[l])
        x = x + (jax.nn.silu(h2 @ w_gate[l]) * (h2 @ w_up[l])) @ w_down[l]

    return rmsnorm(x, g_final)
```

```python
import numpy as np
import concourse.bass as bass
import concourse.mybir as mybir
from concourse.bass_utils import run_bass_kernel_spmd

F32 = mybir.dt.float32
BF16 = mybir.dt.bfloat16
AF = mybir.ActivationFunctionType
ALU = mybir.AluOpType

D = 4096
S = 2048
T = 1024
NB = 512
EPS = 1e-6
DFF = 11008
NFF = 86
MLA_SCALE = 192.0 ** -0.5
GQA_SCALE = 128.0 ** -0.5
THETA = 10000.0
FF_PARTS = [(0, 22), (22, 44), (44, 65), (65, 86)]
WD_PAD = 22 * 128
ARENA_WORDS = 53200
KPE = 9

G_ATTN, G_QA, G_KVA, G_QN, G_KN, G_FFN, G_FIN = 0, 32, 40, 44, 45, 46, 78
NG = 110


class TT:
    __slots__ = ("w", "r", "x")

    def __init__(self, x=False):
        self.w = None
        self.r = {}
        self.x = x


def tts(n):
    return [TT() for _ in range(n)]


class Prog:
    ENG = ("pe", "act", "dve", "pool", "sp")

    def __init__(self, nc):
        self.nc = nc
        self.st = {}
        for e in self.ENG:
            self.st[e] = dict(ops=[], sem=None, count=0, waited={}, dsems=[], dma_i=0)
        for e in ("pe", "act", "dve", "pool"):
            self.st[e]["sem"] = nc.alloc_semaphore(name="c_" + e)
        for e, n in (("sp", 8), ("pool", 8), ("act", 4)):
            self.st[e]["dsems"] = [nc.alloc_semaphore(name="d_%s%d" % (e, i)) for i in range(n)]
        self.semkey = {}

    def _key(self, sem):
        return sem.num

    def _wait(self, e, sem, val):
        st = self.st[e]
        if e == "pe" and sem is st["sem"]:
            return
        k = self._key(sem)
        if st["waited"].get(k, 0) >= val:
            return
        st["waited"][k] = val
        st["ops"].append(("w", sem, val))

    def _deps(self, e, reads, writes):
        for t in reads:
            if t.w is not None:
                self._wait(e, t.w[0], t.w[1])
        for t in writes:
            if t.w is not None:
                self._wait(e, t.w[0], t.w[1])
            for ev in t.r.values():
                self._wait(e, ev[0], ev[1])

    def _mark(self, ev, reads, writes):
        k = self._key(ev[0])
        for t in reads:
            t.r[k] = ev
        for t in writes:
            t.w = ev
            t.r = {}

    def op(self, e, fn, reads=(), writes=()):
        st = self.st[e]
        if any(t.x for t in reads):
            writes = tuple(writes) + tuple(t for t in reads if t.x)
            reads = tuple(t for t in reads if not t.x)
        self._deps(e, reads, writes)
        st["count"] += 1
        ev = (st["sem"], st["count"])
        st["ops"].append(("o", fn))
        self._mark(ev, reads, writes)

    def dma(self, q, out, in_, reads=(), writes=()):
        st = self.st[q]
        n = len(st["dsems"])
        i = st["dma_i"]
        st["dma_i"] += 1
        sem = st["dsems"][i % n]
        val = 16 * (i // n + 1)
        if i >= n:
            self._wait(q, sem, val - 16)
        self._deps(q, reads, writes)
        st["ops"].append(("d", out, in_, sem))
        self._mark((sem, val), reads, writes)

    def all_events(self):
        evs = []
        for e in self.ENG:
            st = self.st[e]
            if st["sem"] is not None and st["count"] > 0:
                evs.append((st["sem"], st["count"]))
            n = len(st["dsems"])
            for j, sem in enumerate(st["dsems"]):
                cnt = (st["dma_i"] - j + n - 1) // n if st["dma_i"] > j else 0
                if cnt > 0:
                    evs.append((sem, 16 * cnt))
        return evs

    def barrier(self):
        evs = self.all_events()
        for e in self.ENG:
            for sem, val in evs:
                self._wait(e, sem, val)

    def finish(self):
        for sem, val in self.all_events():
            self._wait("sp", sem, val)

    def replay(self, e, eng):
        st = self.st[e]
        sem = st["sem"]
        for o in st["ops"]:
            if o[0] == "w":
                eng.wait_ge(o[1], o[2])
            elif o[0] == "o":
                o[1](eng).then_inc(sem, 1)
            else:
                eng.dma_start(out=o[1], in_=o[2]).then_inc(o[3], 16)


class Arena:
    def __init__(self, big):
        self.big = big
        self.top = 0
        self.peak = 0

    def _alloc(self, nbytes):
        off = self.top
        self.top += (nbytes + 63) // 64 * 64
        assert self.top <= getattr(self, "limit", ARENA_WORDS * 4), ("SBUF arena overflow", self.top)
        self.peak = max(self.peak, self.top)
        return off

    def _view(self, shape, dt, esz):
        n = 1
        for s_ in shape[1:]:
            n *= s_
        off = self._alloc(n * esz)
        ap = self.big[0:shape[0], off // 4: off // 4 + (n * esz) // 4]
        if dt is not F32:
            ap = ap.bitcast(dt)
        if len(shape) == 3:
            ap = ap.rearrange("p (a b) -> p a b", a=shape[1])
        elif len(shape) == 4:
            ap = ap.rearrange("p (a b c) -> p a b c", a=shape[1], b=shape[2])
        return ap

    def f32(self, shape):
        return self._view(shape, F32, 4)

    def bf16(self, shape):
        return self._view(shape, BF16, 2)

    def sub(self, off, nbytes):
        a = Arena(self.big)
        a.top = off
        a.limit = off + nbytes
        return a

    def mark(self):
        return self.top

    def release(self, m):
        self.top = m


class Pool_:
    def __init__(self, aps):
        self.aps = aps
        self.t = tts(len(aps))
        self.i = 0

    def get(self):
        j = self.i % len(self.aps)
        self.i += 1
        return self.aps[j], self.t[j]


class Ring:
    def __init__(self, P, A, nslots):
        self.P = P
        self.buf = A.bf16([128, nslots, 4096])
        self.t = tts(nslots)
        self.n = nslots
        self.i = 0

    def fetch(self, src, nelem):
        j = self.i % self.n
        self.i += 1
        self.P.dma("pool", self.buf[:, j, 0:nelem], src, reads=(), writes=(self.t[j],))
        return self.buf[:, j, 0:nelem], self.t[j]


def i_act(out, in_, func, scale=None):
    if scale is None:
        return lambda e: e.activation(out=out, in_=in_, func=func)
    return lambda e: e.activation(out=out, in_=in_, func=func, scale=scale)


def i_ts(out, in0, s1, s2, op0, op1):
    return lambda e: e.tensor_scalar(out=out, in0=in0, scalar1=s1, scalar2=s2, op0=op0, op1=op1)


def i_stt(out, in0, scalar, in1, op0=ALU.mult, op1=ALU.mult):
    return lambda e: e.scalar_tensor_tensor(out=out, in0=in0, scalar=scalar, in1=in1, op0=op0, op1=op1)


def i_tt(out, in0, in1, op):
    return lambda e: e.tensor_tensor(out=out, in0=in0, in1=in1, op=op)


def i_copy(out, in_):
    return lambda e: e.tensor_copy(out=out, in_=in_)


def i_recip(out, in_):
    return lambda e: e.reciprocal(out=out, in_=in_)


def i_mm(out, pairs, start=True, stop=True):
    def fn(t):
        n = len(pairs)
        ins = None
        for i, (l, r) in enumerate(pairs):
            ins = t.matmul(out, l, r, start=(start and i == 0), stop=(stop and i == n - 1))
        return ins
    return fn


def i_tr(outs_ins, ident):
    def fn(t):
        ins = None
        for o, i_ in outs_ins:
            ins = t.transpose(o, i_, ident)
        return ins
    return fn


def pipeline(items, stages):
    n = len(items)
    maxlag = max(l for _, l in stages)
    for s_ in range(n + maxlag):
        for fn, lag in stages:
            j = s_ - lag
            if 0 <= j < n:
                fn(items[j])


def build_program(stages=(1, 2, 3, 4, 5), dbg=False):
    nc = bass.Bass("TRN2", target_bir_lowering=False)

    def din(name, shape):
        return nc.dram_tensor(name, list(shape), F32, kind="ExternalInput").ap()

    def dscr(name, shape, dt):
        return nc.dram_tensor(name, list(shape), dt, kind="ExternalOutput" if dbg else "Internal").ap()

    xin = din("xin", [128, 32, S])
    tabs = din("tabs", [128, 4, S])
    cmat = din("cmat", [128, 3, 128])
    gains = din("gains", [128, NG])
    wkv = din("wkv", [13, 128, 4096])
    wq = din("wq", [24, 128, 4096])
    wqb = wkvb = None
    if 2 in stages:
        wqb = din("wqb", [16, 128, 2048])
        wkvb = din("wkvb", [4, 128, 4096])
    wg = wab = wo = wgu = wd = None
    if 3 in stages:
        wg = din("wg", [64, 128, 4096])
        wab = din("wab", [32, 128, 4096])
    if 4 in stages:
        wo = din("wo", [32, 128, 4096])
    if 5 in stages:
        wgu = din("wgu", [172, 128, 4096])
        wd = din("wd", [len(FF_PARTS) * 32, 128, WD_PAD])
    y = nc.dram_tensor("y", [128, 32, T], F32, kind="ExternalOutput").ap()

    ckv_s = dscr("ckv_s", [128, 4, S], BF16)
    kpe_s = dscr("kpe_s", [128, S], BF16)
    cq_s = dscr("cq_s", [128, 8, T], BF16)
    qb_s = dscr("qb_s", [128, 16, T], BF16)
    hs_s = dscr("hs_s", [128, 32, T], BF16)
    m_s = dscr("m_s", [128, 32, T], BF16)
    x2_s = dscr("x2_s", [128, 32, T], F32)
    acc_s = dscr("acc_s", [128, 32, T], F32)
    if dbg:
        kb_d = dscr("kb_d", [128, 4, S], BF16)
        vb_d = dscr("vb_d", [128, 16, 512], BF16)
        o_d = dscr("o_d", [128, 32, T], BF16)
    ckv_st = tts(1)
    kpe_st = tts(1)
    cq_st = tts(1)
    qb_st = tts(16)
    hs_st = tts(1)
    m_st = tts(32)
    x2_st = [tts(2) for _ in range(32)]
    acc_st = [tts(2) for _ in range(32)]

    big = nc.alloc_sbuf_tensor("big", [128, ARENA_WORDS], F32) if hasattr(nc, "alloc_sbuf_tensor") else None
    ctx_big = None
    if big is None:
        ctx_big = nc.sbuf_tensor("big", [128, ARENA_WORDS], F32)
        big = ctx_big.__enter__()
    ctx_ps = nc.psum_tensor("ps", [128, 8, 512], F32)
    ps = ctx_ps.__enter__()
    psT = [TT(x=True) for _ in range(8)]

    P = Prog(nc)
    A = Arena(big)

    cm = A.bf16([128, 3, 128])
    cm_t = TT()
    gn = A.f32([128, NG])
    gn_t = TT()
    ones = A.bf16([128, 128])
    ones_t = TT()
    P.dma("pool", cm, cmat, writes=(cm_t,))
    P.dma("sp", gn, gains, writes=(gn_t,))
    P.op("dve", lambda e: e.memset(ones, 1.0), writes=(ones_t,))
    ident = cm[:, 0, :]
    perm_g = cm[:, 1, :]
    perm_m = cm[:, 2, :]
    CT = (cm_t, gn_t, ones_t)

    def rstd_ops(dst, dst_t, src_ps, src_t, n, np_=128):
        P.op("dve", i_ts(dst[0:np_], src_ps, 1.0 / n, EPS, ALU.mult, ALU.add), reads=(src_t,), writes=(dst_t,))
        P.op("act", i_act(dst[0:np_], dst[0:np_], AF.Sqrt), reads=(dst_t,), writes=(dst_t,))
        P.op("dve", i_recip(dst[0:np_], dst[0:np_]), reads=(dst_t,), writes=(dst_t,))

    rstd2 = A.f32([128, 2, NB])
    rstd2_t = tts(2)
    rstd3 = A.f32([128, 2, NB])
    rstd3_t = tts(2)
    m_glob = A.mark()

    a64_off = A.mark()
    oT = A.bf16([128, 32, T])
    o_t = [tts(2) for _ in range(32)]
    A1 = A.sub(a64_off, 65536)
    kbT = A.bf16([128, 4, S])
    kbT_t = tts(4)
    vb = A.bf16([128, 16, 512])
    vb_t = tts(16)
    m_kv = A.mark()

    if 1 in stages:
        ring = Ring(P, A, 6)
        hT = A1.bf16([128, 32, NB])
        hT_t = tts(32)
        xst = [A.f32([128, 2, NB]) for _ in range(2)]
        xst_t = tts(2)
        sqx = [A.bf16([128, 2, NB]) for _ in range(2)]
        sqx_t = tts(2)
        rx = A.f32([128, NB])
        rx_t = TT()
        tab = A1.f32([128, 4, NB])
        tab_t = TT()
        raw_kv = A1.f32([128, 4, NB])
        raw_kv_t = tts(4)
        raw_qa = A1.f32([128, 8, NB])
        raw_qa_t = tts(8)
        zv16 = A.bf16([128, 4, NB])
        zv16_t = tts(4)
        RAWP = Pool_([A.f32([128, NB]) for _ in range(4)])
        TF = Pool_([A.f32([128, NB]) for _ in range(4)])
        SQP = Pool_([A.bf16([128, NB]) for _ in range(4)])
        K16P = Pool_([A.bf16([128, NB]) for _ in range(4)])
        TB = Pool_([A.bf16([128, NB]) for _ in range(4)])
        PB = [0, 1, 2, 3, 4]
        SBK, SB2, SB3 = 5, 6, 7
        pbi = [0]
        ps7b = ps[:, SB3, :].bitcast(BF16)

        def p1_cg(tb_, cg):
            t0_ = tb_ * NB
            b = cg % 2
            P.dma("sp", xst[b], xin[:, 2 * cg:2 * cg + 2, t0_:t0_ + NB], writes=(xst_t[b],))
            P.op("act", i_act(sqx[b], xst[b], AF.Square), reads=(xst_t[b],), writes=(sqx_t[b],))
            P.op("pe", i_mm(ps[:, SBK, :], [(ones, sqx[b][:, 0, :]), (ones, sqx[b][:, 1, :])],
                            start=(cg == 0), stop=(cg == 15)),
                 reads=(sqx_t[b], ones_t), writes=(psT[SBK],))

        def p2(tb_):
            t0_ = tb_ * NB
            o0_ = (tb_ - 2) * NB
            rstd_ops(rx, rx_t, ps[:, SBK, :], psT[SBK], D)
            for cg in range(16):
                b = cg % 2
                P.dma("sp", xst[b], xin[:, 2 * cg:2 * cg + 2, t0_:t0_ + NB], writes=(xst_t[b],))
                for i in range(2):
                    c = 2 * cg + i
                    P.op("dve", i_stt(hT[:, c, :], xst[b][:, i, :], gn[:, G_ATTN + c:G_ATTN + c + 1], rx),
                         reads=(xst_t[b], rx_t, gn_t), writes=(hT_t[c],))
            if tb_ >= 2:
                for c0 in range(0, 32, 8):
                    P.dma("sp", hs_s[:, c0:c0 + 8, o0_:o0_ + NB], hT[:, c0:c0 + 8, :],
                          reads=tuple(hT_t[c0:c0 + 8]), writes=(hs_st[0],))

        pend_p1 = []
        for cg in range(16):
            p1_cg(0, cg)
        for tb in range(4):
            own = tb >= 2
            t0 = tb * NB
            o0 = (tb - 2) * NB
            p2(tb)
            P.dma("sp", tab, tabs[:, :, t0:t0 + NB], writes=(tab_t,))
            if tb + 1 < 4:
                pend_p1 = [(tb + 1, cg) for cg in range(16)]

            items = []
            for j in range(4):
                items.append(("kva", j, wkv[j], 128))
            for j in range(4):
                items.append(("k", j, wkv[4 + j], 128))
            for j in range(4):
                items.append(("v", j, wkv[8 + j], 128))
            items.append(("kpe", 0, wkv[12], 128))
            if own:
                for j in range(8):
                    items.append(("qa", j, wq[j], 128))
                for j in range(16):
                    items.append(("q", j, wq[8 + j], 128))
            items = [dict(kind=k, j=j, src=src, nb=nb) for (k, j, src, nb) in items]

            def st_proj(it):
                nb = it["nb"]
                sl, sl_t = ring.fetch(it["src"][:, 0:32 * nb], 32 * nb)
                w = sl.rearrange("p (c n) -> p c n", c=32)
                bank = PB[pbi[0] % len(PB)]
                pbi[0] += 1
                it["bank"] = bank
                P.op("pe", i_mm(ps[0:nb, bank, :], [(w[:, kc, :], hT[:, kc, :]) for kc in range(32)]),
                     reads=(sl_t,) + tuple(hT_t), writes=(psT[bank],))
                for _ in range(2):
                    if pend_p1:
                        p1_cg(*pend_p1.pop(0))

            def st_a(it, tb=tb, t0=t0, o0=o0):
                k, j, bank = it["kind"], it["j"], it["bank"]
                src = ps[:, bank, :]
                bt = psT[bank]
                if k == "kva":
                    P.op("act", i_act(raw_kv[:, j, :], src, AF.Copy), reads=(bt,), writes=(raw_kv_t[j],))
                    sq, sq_t = SQP.get()
                    P.op("act", i_act(sq, src, AF.Square), reads=(bt,), writes=(sq_t,))
                    it["sq"] = (sq, sq_t)
                elif k == "qa":
                    P.op("act", i_act(raw_qa[:, j, :], src, AF.Copy), reads=(bt,), writes=(raw_qa_t[j],))
                    sq, sq_t = SQP.get()
                    P.op("act", i_act(sq, src, AF.Square), reads=(bt,), writes=(sq_t,))
                    it["sq"] = (sq, sq_t)
                elif k in ("k", "q"):
                    raw, raw_t = RAWP.get()
                    P.op("act", i_act(raw, src, AF.Copy), reads=(bt,), writes=(raw_t,))
                    sq, sq_t = SQP.get()
                    P.op("act", i_act(sq, src, AF.Square), reads=(bt,), writes=(sq_t,))
                    it["raw"] = (raw, raw_t)
                    it["sq"] = (sq, sq_t)
                elif k == "v":
                    P.op("act", i_act(zv16[:, j, :], src, AF.Copy), reads=(bt,), writes=(zv16_t[j],))
                elif k == "kpe" and KPE >= 1:
                    raw, raw_t = RAWP.get()
                    P.op("act", i_act(raw, ps[:, bank, :], AF.Copy), reads=(bt,), writes=(raw_t,))
                    k16, k16_t = K16P.get()
                    P.op("dve", i_copy(k16, ps[:, bank, :]), reads=(bt,), writes=(k16_t,))
                    it["raw"] = (raw, raw_t)
                    it["k16"] = (k16, k16_t)

            def st_b(it):
                k, j = it["kind"], it["j"]
                if k in ("kva", "qa"):
                    last = 3 if k == "kva" else 7
                    sq, sq_t = it["sq"]
                    P.op("pe", i_mm(ps[:, SB2, :], [(ones, sq)], start=(j == 0), stop=(j == last)),
                         reads=(sq_t, ones_t), writes=(psT[SB2],))
                elif k in ("k", "q"):
                    sq, sq_t = it["sq"]
                    P.op("pe", i_mm(ps[:, SB2, :], [(ones, sq)]), reads=(sq_t, ones_t), writes=(psT[SB2],))
                elif k == "v" and j == 3:
                    for tc in range(4):
                        P.op("pe", i_tr([(ps7b[:, jj * 128:(jj + 1) * 128], zv16[:, jj, tc * 128:(tc + 1) * 128])
                                         for jj in range(4)], ident),
                             reads=tuple(zv16_t) + (cm_t,), writes=(psT[SB3],))
                        kc = tb_cur[0] * 4 + tc
                        P.op("dve", i_copy(vb[:, kc, :], ps7b[:, 0:512]), reads=(psT[SB3],), writes=(vb_t[kc],))

            def st_c(it, t0=t0, o0=o0):
                k, j = it["kind"], it["j"]
                if k == "kva" and j == 3:
                    rs, rs_t = TF.get()
                    rstd_ops(rs, rs_t, ps[:, SB2, :], psT[SB2], 512)
                    for jj in range(4):
                        o16, o16_t = TB.get()
                        P.op("dve", i_stt(o16, raw_kv[:, jj, :], gn[:, G_KVA + jj:G_KVA + jj + 1], rs),
                             reads=(raw_kv_t[jj], rs_t, gn_t), writes=(o16_t,))
                        P.dma("sp", ckv_s[:, jj, t0:t0 + NB], o16, reads=(o16_t,), writes=(ckv_st[0],))
                elif k == "qa" and j == 7:
                    rs, rs_t = TF.get()
                    rstd_ops(rs, rs_t, ps[:, SB2, :], psT[SB2], 1024)
                    for jj in range(8):
                        o16, o16_t = TB.get()
                        P.op("dve", i_stt(o16, raw_qa[:, jj, :], gn[:, G_QA + jj:G_QA + jj + 1], rs),
                             reads=(raw_qa_t[jj], rs_t, gn_t), writes=(o16_t,))
                        P.dma("sp", cq_s[:, jj, o0:o0 + NB], o16, reads=(o16_t,), writes=(cq_st[0],))
                elif k in ("k", "q"):
                    rs, rs_t = TF.get()
                    rstd_ops(rs, rs_t, ps[:, SB2, :], psT[SB2], 128)
                    raw, raw_t = it["raw"]
                    gcol = G_KN if k == "k" else G_QN
                    P.op("dve", i_stt(raw, raw, gn[:, gcol:gcol + 1], rs), reads=(raw_t, rs_t, gn_t), writes=(raw_t,))
                    k16, k16_t = K16P.get()
                    P.op("act", i_act(k16, raw, AF.Copy), reads=(raw_t,), writes=(k16_t,))
                    it["k16"] = (k16, k16_t)

            def st_d(it):
                k = it["kind"]
                if k in ("k", "q"):
                    k16, k16_t = it["k16"]
                    P.op("pe", i_mm(ps[:, SB3, :], [(perm_g, k16)]), reads=(k16_t, cm_t), writes=(psT[SB3],))
                elif k == "kpe" and KPE >= 2:
                    k16, k16_t = it["k16"]
                    P.op("pe", i_mm(ps[:, SB3, :], [(perm_m, k16)]), reads=(k16_t, cm_t), writes=(psT[SB3],))

            def st_e(it, t0=t0, o0=o0):
                k, j = it["kind"], it["j"]
                if k in ("k", "q", "kpe") and (k != "kpe" or KPE >= 3):
                    np_ = 128
                    ci, si = (2, 3) if k == "kpe" else (0, 1)
                    raw, raw_t = it["raw"]
                    t2, t2_t = TF.get()
                    P.op("dve", i_tt(t2[0:np_], ps[0:np_, SB3, :], tab[0:np_, si, :], ALU.mult),
                         reads=(psT[SB3], tab_t), writes=(t2_t,))
                    P.op("dve", i_tt(raw[0:np_], raw[0:np_], tab[0:np_, ci, :], ALU.mult),
                         reads=(raw_t, tab_t), writes=(raw_t,))
                    if k == "k":
                        P.op("dve", i_tt(kbT[:, j, t0:t0 + NB], raw, t2, ALU.add),
                             reads=(raw_t, t2_t), writes=(kbT_t[tb_cur[0]],))
                    else:
                        o16, o16_t = TB.get()
                        P.op("dve", i_tt(o16[0:np_], raw[0:np_], t2[0:np_], ALU.add),
                             reads=(raw_t, t2_t), writes=(o16_t,))
                        if k == "q":
                            P.dma("sp", qb_s[:, j, o0:o0 + NB], o16, reads=(o16_t,), writes=(qb_st[j],))
                        else:
                            if KPE >= 4:
                                P.dma("sp", kpe_s[:, t0:t0 + NB], o16, reads=(o16_t,), writes=(kpe_st[0],))

            tb_cur = [tb]
            pipeline(items, [(st_proj, 0), (st_a, 0), (st_b, 1), (st_c, 1), (st_d, 2), (st_e, 2)])
            while pend_p1:
                p1_cg(*pend_p1.pop(0))

        if dbg:
            for tbb in range(4):
                P.dma("sp", kb_d[:, :, tbb * NB:(tbb + 1) * NB], kbT[:, :, tbb * NB:(tbb + 1) * NB],
                      reads=(kbT_t[tbb],))
            P.dma("sp", vb_d, vb, reads=tuple(vb_t))
        P.barrier()
    A.release(m_kv)


    if 2 in stages:
        def attn_phase(tasks, PT, PT_t, rec, rec_t):
            steps = []
            for ti, tk in enumerate(tasks):
                for j in range(8):
                    steps.append((ti, j))
            n = len(steps)
            ptc = [0]
            info = {}

            def qk(si):
                ti, j = steps[si]
                tk = tasks[ti]
                if j == 0 and tk.get("pre") is not None:
                    tk["pre"]()
                sb = si % 2
                for i in range(2):
                    kc = 2 * j + i
                    P.op("pe", i_mm(ps[:, 2 * sb + i, :], tk["qk"](kc)), reads=tk["rq"], writes=(psT[2 * sb + i],))
                pt, pt_t = PT[si % 3], PT_t[si % 3]
                P.op("act", i_act(pt, ps[:, 2 * sb:2 * sb + 2, :], AF.Exp, scale=tk["scale"]),
                     reads=(psT[2 * sb], psT[2 * sb + 1]), writes=(pt_t,))

            def pv(si):
                ti, j = steps[si]
                tk = tasks[ti]
                ob, db = 4 + ti % 2, 6 + ti % 2
                pt, pt_t = PT[si % 3], PT_t[si % 3]
                mm_o = []
                for i in range(2):
                    kc = 2 * j + i
                    mm_o.append((tk["v"](kc), pt[:, i, :]))

                def fn(t, mm_o=mm_o, j=j, ob=ob, db=db, pt=pt):
                    ins = None
                    for i in range(2):
                        kc = 2 * j + i
                        t.matmul(ps[:, ob, :], mm_o[i][0], mm_o[i][1], start=(kc == 0), stop=(kc == 15))
                        ins = t.matmul(ps[:, db, :], ones, pt[:, i, :], start=(kc == 0), stop=(kc == 15))
                    return ins
                P.op("pe", fn, reads=(pt_t, ones_t) + tk["rv"], writes=(psT[ob], psT[db]))
                if j == 7:
                    r, r_t = rec[ti % 2], rec_t[ti % 2]
                    P.op("dve", i_recip(r, ps[:, db, :]), reads=(psT[db],), writes=(r_t,))
                    P.op("dve", i_tt(tk["dst"], ps[:, ob, :], r, ALU.mult), reads=(psT[ob], r_t), writes=(tk["dst_t"],))
                    if tk.get("post") is not None:
                        tk["post"]()

            pending = None
            for si in range(n):
                ti, j = steps[si]
                if j == 0 and tasks[ti].get("flush") and pending is not None:
                    pv(pending)
                    pending = None
                qk(si)
                if pending is not None:
                    pv(pending)
                pending = si
            pv(pending)

        m2 = A.mark()
        PT = [A.bf16([128, 2, NB]) for _ in range(3)]
        PT_t = tts(3)
        rec = [A.f32([128, NB]) for _ in range(2)]
        rec_t = tts(2)
        qh = [A.bf16([128, T]) for _ in range(2)]
        qh_t = tts(2)
        tasks = []
        for h in range(16):
            g = h // 4
            qbuf, qbuf_t = qh[h % 2], qh_t[h % 2]
            for qb in range(2):
                def pre(h=h, qbuf=qbuf, qbuf_t=qbuf_t):
                    P.dma("sp", qbuf, qb_s[:, h, :], reads=(qb_st[h],), writes=(qbuf_t,))
                tasks.append(dict(
                    qk=(lambda kc, g=g, qbuf=qbuf, qb=qb: [(kbT[:, g, kc * 128:(kc + 1) * 128], qbuf[:, qb * NB:(qb + 1) * NB])]),
                    v=(lambda kc, g=g: vb[:, kc, g * 128:(g + 1) * 128]),
                    rq=tuple(kbT_t) + (qbuf_t,), rv=tuple(vb_t),
                    dst=oT[:, 16 + h, qb * NB:(qb + 1) * NB], dst_t=o_t[16 + h][qb], scale=GQA_SCALE,
                    pre=(pre if qb == 0 else None)))
        attn_phase(tasks, PT, PT_t, rec, rec_t)
        P.barrier()
        A.release(a64_off + 65536)

        ring = Ring(P, A, 3)
        PT = [A.bf16([128, 2, NB]) for _ in range(3)]
        PT_t = tts(3)
        rec = [A.f32([128, NB]) for _ in range(2)]
        rec_t = tts(2)
        ckvT = A.bf16([128, 4, S])
        ckvT_t = TT()
        kpeT = A.bf16([128, S])
        kpeT_t = TT()
        cqT = A.bf16([128, 8, T])
        cqT_t = TT()
        tabm = A.f32([128, 2, T])
        tabm_t = TT()
        knope = A.bf16([128, 4, S])
        knope_t = tts(4)
        va = A.bf16([128, 16, 512])
        va_t = tts(16)
        qn = [A.bf16([128, T]) for _ in range(2)]
        qn_t = tts(2)
        qp = [A.bf16([128, T]) for _ in range(2)]
        qp_t = tts(2)
        RAW2 = Pool_([A.f32([128, NB]) for _ in range(2)])
        K162 = Pool_([A.bf16([128, NB]) for _ in range(2)])
        T22 = Pool_([A.f32([128, NB]) for _ in range(2)])
        P.dma("sp", ckvT, ckv_s, reads=(ckv_st[0],), writes=(ckvT_t,))
        P.dma("sp", kpeT, kpe_s, reads=(kpe_st[0],), writes=(kpeT_t,))
        P.dma("sp", cqT, cq_s, reads=(cq_st[0],), writes=(cqT_t,))
        P.dma("sp", tabm, tabs[:, 2:4, T:S], writes=(tabm_t,))
        xb = [0]

        def xbank():
            b = xb[0] % 4
            xb[0] += 1
            return b

        def expand_group(hg):
            sl, sl_t = ring.fetch(wkvb[hg], 4096)
            w = sl.rearrange("p (c n) -> p c n", c=4)
            for hl in range(4):
                for tbk in range(4):
                    b = xbank()
                    P.op("pe", i_mm(ps[:, b, :], [(w[:, kc, hl * 128:(hl + 1) * 128], ckvT[:, kc, tbk * NB:(tbk + 1) * NB])
                                                  for kc in range(4)]),
                         reads=(sl_t, ckvT_t), writes=(psT[b],))
                    eng = "act" if (hl * 4 + tbk) % 2 == 0 else "dve"
                    fn = i_act(knope[:, hl, tbk * NB:(tbk + 1) * NB], ps[:, b, :], AF.Copy) if eng == "act" else \
                        i_copy(knope[:, hl, tbk * NB:(tbk + 1) * NB], ps[:, b, :])
                    P.op(eng, fn, reads=(psT[b],), writes=(knope_t[hl],))
            for kc16 in range(16):
                b = xbank()
                P.op("pe", i_mm(ps[:, b, :], [(ckvT[:, kc, kc16 * 128:(kc16 + 1) * 128], w[:, kc, 512:1024])
                                              for kc in range(4)]),
                     reads=(sl_t, ckvT_t), writes=(psT[b],))
                eng = "act" if kc16 % 2 == 0 else "dve"
                fn = i_act(va[:, kc16, :], ps[:, b, :], AF.Copy) if eng == "act" else i_copy(va[:, kc16, :], ps[:, b, :])
                P.op(eng, fn, reads=(psT[b],), writes=(va_t[kc16],))

        prep_state = {}

        def prep_a(h):
            sl, sl_t = ring.fetch(wqb[h], 2048)
            w = sl.rearrange("p (c n) -> p c n", c=8)
            bq = h % 2
            st_ = []
            for qb in range(2):
                qs = slice(qb * NB, (qb + 1) * NB)
                b = xbank()
                P.op("pe", i_mm(ps[:, b, :], [(w[:, kc, 0:128], cqT[:, kc, qs]) for kc in range(8)]),
                     reads=(sl_t, cqT_t), writes=(psT[b],))
                P.op("dve", i_copy(qn[bq][:, qs], ps[:, b, :]), reads=(psT[b],), writes=(qn_t[bq],))
                b = xbank()
                P.op("pe", i_mm(ps[:, b, :], [(w[:, kc, 128:256], cqT[:, kc, qs]) for kc in range(8)]),
                     reads=(sl_t, cqT_t), writes=(psT[b],))
                raw, raw_t = RAW2.get()
                k16, k16_t = K162.get()
                P.op("act", i_act(raw, ps[:, b, :], AF.Copy), reads=(psT[b],), writes=(raw_t,))
                P.op("dve", i_copy(k16, ps[:, b, :]), reads=(psT[b],), writes=(k16_t,))
                st_.append((raw, raw_t, k16, k16_t, qs))
            prep_state[h] = st_

        def prep_b(h):
            bq = h % 2
            for (raw, raw_t, k16, k16_t, qs) in prep_state.pop(h):
                b = xbank()
                P.op("pe", i_mm(ps[:, b, :], [(perm_m, k16)]), reads=(k16_t, cm_t), writes=(psT[b],))
                t2, t2_t = T22.get()
                P.op("dve", i_tt(t2, ps[:, b, :], tabm[:, 1, qs], ALU.mult), reads=(psT[b], tabm_t), writes=(t2_t,))
                P.op("dve", i_tt(raw, raw, tabm[:, 0, qs], ALU.mult), reads=(raw_t, tabm_t), writes=(raw_t,))
                P.op("dve", i_tt(qp[bq][:, qs], raw, t2, ALU.add), reads=(raw_t, t2_t), writes=(qp_t[bq],))

        tasks = []
        for h in range(16):
            hg, hl = h // 4, h % 4
            bq = h % 2
            for qb in range(2):
                qs = slice(qb * NB, (qb + 1) * NB)

                def pre(h=h, hl=hl, hg=hg, qb=qb):
                    if hl == 0 and qb == 0:
                        expand_group(hg)
                    if h == 0 and qb == 0:
                        prep_a(0)
                        prep_b(0)
                    if qb == 0 and h + 1 < 16:
                        prep_a(h + 1)
                    if qb == 1 and h + 1 < 16:
                        prep_b(h + 1)

                tasks.append(dict(
                    qk=(lambda kc, hl=hl, bq=bq, qs=qs: [(knope[:, hl, kc * 128:(kc + 1) * 128], qn[bq][:, qs]),
                                                           (kpeT[:, kc * 128:(kc + 1) * 128], qp[bq][:, qs])]),
                    v=(lambda kc, hl=hl: va[:, kc, hl * 128:(hl + 1) * 128]),
                    rq=(knope_t[hl], kpeT_t, qn_t[bq], qp_t[bq]), rv=tuple(va_t),
                    dst=oT[:, h, qs], dst_t=o_t[h][qb], scale=MLA_SCALE, pre=pre, flush=(hl == 0 and qb == 0)))
        attn_phase(tasks, PT, PT_t, rec, rec_t)
        if dbg:
            P.dma("sp", o_d, oT, reads=tuple(t for pr in o_t for t in pr))
        P.barrier()
        A.release(a64_off + 65536)


    def qsl(qb):
        return slice(qb * NB, (qb + 1) * NB)

    if 3 in stages:
        ring = Ring(P, A, 6)
        hB = A.bf16([128, 32, T])
        hB_t = TT()
        for c0 in range(0, 32, 8):
            P.dma("sp", hB[:, c0:c0 + 8, :], hs_s[:, c0:c0 + 8, :], reads=(hs_st[0],), writes=(hB_t,))
        SAp = Pool_([A.f32([128, NB]) for _ in range(2)])
        SBp = Pool_([A.f32([128, NB]) for _ in range(2)])
        TMp = Pool_([A.f32([128, NB]) for _ in range(2)])
        MSp = Pool_([A.bf16([128, NB]) for _ in range(3)])
        o_all = tuple(t for pr in o_t for t in pr)
        it3 = 0
        for c in range(32):
            slA, slA_t = ring.fetch(wg[2 * c], 4096)
            slB, slB_t = ring.fetch(wg[2 * c + 1], 4096)
            slW, slW_t = ring.fetch(wab[c], 4096)
            wA = slA.rearrange("p (c n) -> p c n", c=32)
            wB = slB.rearrange("p (c n) -> p c n", c=32)
            wW = slW.rearrange("p (c n) -> p c n", c=32)
            for qb in range(2):
                qs = qsl(qb)
                b0 = 4 * (it3 % 2)
                it3 += 1
                P.op("pe", i_mm(ps[:, b0, :], [(wA[:, kc, :], hB[:, kc, qs]) for kc in range(32)]),
                     reads=(slA_t, hB_t), writes=(psT[b0],))
                P.op("pe", i_mm(ps[:, b0 + 1, :], [(wB[:, kc, :], hB[:, kc, qs]) for kc in range(32)]),
                     reads=(slB_t, hB_t), writes=(psT[b0 + 1],))
                P.op("pe", i_mm(ps[:, b0 + 2, :], [(wW[:, kc, :], oT[:, kc, qs]) for kc in range(16)]),
                     reads=(slW_t,) + o_all, writes=(psT[b0 + 2],))
                P.op("pe", i_mm(ps[:, b0 + 3, :], [(wW[:, 16 + kc, :], oT[:, 16 + kc, qs]) for kc in range(16)]),
                     reads=(slW_t,) + o_all, writes=(psT[b0 + 3],))
                sa, sa_t = SAp.get()
                sb_, sb_t = SBp.get()
                tm, tm_t = TMp.get()
                ms, ms_t = MSp.get()
                P.op("act", i_act(sa, ps[:, b0, :], AF.Sigmoid), reads=(psT[b0],), writes=(sa_t,))
                P.op("act", i_act(sb_, ps[:, b0 + 1, :], AF.Sigmoid), reads=(psT[b0 + 1],), writes=(sb_t,))
                P.op("dve", i_tt(tm, ps[:, b0 + 2, :], sa, ALU.mult), reads=(psT[b0 + 2], sa_t), writes=(tm_t,))
                P.op("dve", i_tt(sb_, ps[:, b0 + 3, :], sb_, ALU.mult), reads=(psT[b0 + 3], sb_t), writes=(sb_t,))
                P.op("dve", i_tt(ms, tm, sb_, ALU.add), reads=(tm_t, sb_t), writes=(ms_t,))
                P.dma("sp", m_s[:, c, qs], ms, reads=(ms_t,), writes=(m_st[c],))
        P.barrier()
    A.release(m_glob)

    def stats_tail(ps_banks, dst, dst_t):
        for qb in range(2):
            rstd_ops(dst[:, qb, :], dst_t[qb], ps[:, ps_banks[qb], :], psT[ps_banks[qb]], D)

    if 4 in stages:
        ring = Ring(P, A, 6)
        mB = A.bf16([128, 32, T])
        mB_t = tts(32)
        for c in range(32):
            P.dma("sp", mB[:, c, :], m_s[:, c, :], reads=(m_st[c],), writes=(mB_t[c],))
        XR = Pool_([A.f32([128, T]) for _ in range(2)])
        X2 = Pool_([A.f32([128, NB]) for _ in range(4)])
        SQ = Pool_([A.bf16([128, NB]) for _ in range(4)])
        pend = []
        bi = 0
        for c in range(32):
            sl, sl_t = ring.fetch(wo[c], 4096)
            w = sl.rearrange("p (c n) -> p c n", c=32)
            xr, xr_t = XR.get()
            P.dma("sp", xr, xin[:, c, T:S], writes=(xr_t,))
            newp = []
            for qb in range(2):
                qs = qsl(qb)
                b = bi % 6
                bi += 1
                P.op("pe", i_mm(ps[:, b, :], [(w[:, kc, :], mB[:, kc, qs]) for kc in range(32)]),
                     reads=(sl_t,) + tuple(mB_t), writes=(psT[b],))
                x2, x2_t = X2.get()
                P.op("dve", i_tt(x2, ps[:, b, :], xr[:, qs], ALU.add), reads=(psT[b], xr_t), writes=(x2_t,))
                P.dma("sp", x2_s[:, c, qs], x2, reads=(x2_t,), writes=(x2_st[c][qb],))
                sq, sq_t = SQ.get()
                P.op("act", i_act(sq, x2, AF.Square), reads=(x2_t,), writes=(sq_t,))
                newp.append((qb, c, sq, sq_t))
            for (qb, cc, sq, sq_t) in pend:
                P.op("pe", i_mm(ps[:, 6 + qb, :], [(ones, sq)], start=(cc == 0), stop=(cc == 31)),
                     reads=(sq_t, ones_t), writes=(psT[6 + qb],))
            pend = newp
        for (qb, cc, sq, sq_t) in pend:
            P.op("pe", i_mm(ps[:, 6 + qb, :], [(ones, sq)], start=(cc == 0), stop=(cc == 31)),
                 reads=(sq_t, ones_t), writes=(psT[6 + qb],))
        stats_tail((6, 7), rstd2, rstd2_t)
        P.barrier()
    A.release(m_glob)

    if 5 in stages:
        ring = Ring(P, A, 5)
        h2B = A.bf16([128, 32, T])
        h2_t = tts(32)
        actT = A.bf16([128, 22, T])
        act_t = [tts(2) for _ in range(22)]
        XL = Pool_([A.f32([128, T]) for _ in range(4)])
        SL = Pool_([A.f32([128, NB]) for _ in range(2)])
        OS = Pool_([A.f32([128, NB]) for _ in range(4)])
        SQ = Pool_([A.bf16([128, NB]) for _ in range(4)])
        for c in range(32):
            xl, xl_t = XL.get()
            P.dma("sp", xl, x2_s[:, c, :], reads=tuple(x2_st[c]), writes=(xl_t,))
            P.op("dve", i_stt(h2B[:, c, :], xl, gn[:, G_FFN + c:G_FFN + c + 1], rstd2.rearrange("p a b -> p (a b)")),
                 reads=(xl_t, rstd2_t[0], rstd2_t[1], gn_t), writes=(h2_t[c],))
        bi = 0
        for pi, (fa, fb) in enumerate(FF_PARTS):
            nk = fb - fa
            last = pi == len(FF_PARTS) - 1
            for fl in range(nk):
                f = fa + fl
                slg, slg_t = ring.fetch(wgu[2 * f], 4096)
                slu, slu_t = ring.fetch(wgu[2 * f + 1], 4096)
                wg_ = slg.rearrange("p (c n) -> p c n", c=32)
                wu_ = slu.rearrange("p (c n) -> p c n", c=32)
                for qb in range(2):
                    qs = qsl(qb)
                    bg = 2 * (bi % 3)
                    bi += 1
                    P.op("pe", i_mm(ps[:, bg, :], [(wg_[:, kc, :], h2B[:, kc, qs]) for kc in range(32)]),
                         reads=(slg_t,) + tuple(h2_t), writes=(psT[bg],))
                    P.op("pe", i_mm(ps[:, bg + 1, :], [(wu_[:, kc, :], h2B[:, kc, qs]) for kc in range(32)]),
                         reads=(slu_t,) + tuple(h2_t), writes=(psT[bg + 1],))
                    sl_, sl_t = SL.get()
                    P.op("act", i_act(sl_, ps[:, bg, :], AF.Silu), reads=(psT[bg],), writes=(sl_t,))
                    P.op("dve", i_tt(actT[:, fl, qs], ps[:, bg + 1, :], sl_, ALU.mult),
                         reads=(psT[bg + 1], sl_t), writes=(act_t[fl][qb],))
            pend = []
            for c in range(32):
                sld, sld_t = ring.fetch(wd[pi * 32 + c][:, 0:nk * 128], nk * 128)
                w = sld.rearrange("p (c n) -> p c n", c=nk)
                xl, xl_t = XL.get()
                if pi == 0:
                    P.dma("sp", xl, x2_s[:, c, :], reads=tuple(x2_st[c]), writes=(xl_t,))
                else:
                    P.dma("sp", xl, acc_s[:, c, :], reads=tuple(acc_st[c]), writes=(xl_t,))
                newp = []
                for qb in range(2):
                    qs = qsl(qb)
                    b = bi % 6
                    bi += 1
                    P.op("pe", i_mm(ps[:, b, :], [(w[:, k, :], actT[:, k, qs]) for k in range(nk)]),
                         reads=(sld_t,) + tuple(act_t[k][qb] for k in range(nk)), writes=(psT[b],))
                    os_, os_t = OS.get()
                    P.op("dve", i_tt(os_, ps[:, b, :], xl[:, qs], ALU.add), reads=(psT[b], xl_t), writes=(os_t,))
                    P.dma("sp", acc_s[:, c, qs], os_, reads=(os_t,), writes=(acc_st[c][qb],))
                    if last:
                        sq, sq_t = SQ.get()
                        P.op("act", i_act(sq, os_, AF.Square), reads=(os_t,), writes=(sq_t,))
                        newp.append((qb, c, sq, sq_t))
                for (qb, cc, sq, sq_t) in pend:
                    P.op("pe", i_mm(ps[:, 6 + qb, :], [(ones, sq)], start=(cc == 0), stop=(cc == 31)),
                         reads=(sq_t, ones_t), writes=(psT[6 + qb],))
                pend = newp
            for (qb, cc, sq, sq_t) in pend:
                P.op("pe", i_mm(ps[:, 6 + qb, :], [(ones, sq)], start=(cc == 0), stop=(cc == 31)),
                     reads=(sq_t, ones_t), writes=(psT[6 + qb],))
        stats_tail((6, 7), rstd3, rstd3_t)
        P.barrier()
        A.release(m_glob)
        XL = Pool_([A.f32([128, T]) for _ in range(8)])
        OS = Pool_([A.f32([128, T]) for _ in range(8)])
        r3 = rstd3.rearrange("p a b -> p (a b)")
        for c in range(32):
            xl, xl_t = XL.get()
            P.dma("sp", xl, acc_s[:, c, :], reads=tuple(acc_st[c]), writes=(xl_t,))
            os_, os_t = OS.get()
            P.op("dve", i_stt(os_, xl, gn[:, G_FIN + c:G_FIN + c + 1], r3),
                 reads=(xl_t, rstd3_t[0], rstd3_t[1], gn_t), writes=(os_t,))
            P.dma("act", y[:, c, :], os_, reads=(os_t,))
    P.finish()

    with nc.Block() as block:
        @block.tensor
        def _(e):
            P.replay("pe", e)

        @block.scalar
        def _(e):
            P.replay("act", e)

        @block.vector
        def _(e):
            P.replay("dve", e)

        @block.gpsimd
        def _(e):
            P.replay("pool", e)

        @block.sync
        def _(e):
            P.replay("sp", e)

    print("SBUF peak bytes/partition:", A.peak, "ops:", {e: len(P.st[e]["ops"]) for e in P.ENG})
    return nc


def _blk(W, c0, nb, pad=None):
    K = W.shape[0]
    kc = K // 128
    sub = W[:, c0:c0 + nb].reshape(kc, 128, nb).transpose(1, 0, 2).reshape(128, kc * nb)
    if pad is not None and pad > kc * nb:
        out = np.zeros((128, pad), np.float32)
        out[:, :kc * nb] = sub
        return out
    return np.ascontiguousarray(sub)


def _rope_tables():
    pos = np.arange(S)
    row = (pos // 64).astype(np.float32)
    col = (pos % 64).astype(np.float32)
    tabs = np.zeros((128, 4, S), np.float32)
    inv64 = (THETA ** (-np.arange(0, 64, 2, dtype=np.float32) / 64)).astype(np.float32)
    inv32 = (THETA ** (-np.arange(0, 32, 2, dtype=np.float32) / 32)).astype(np.float32)
    for d in range(128):
        p_ = row if d < 64 else col
        ang = (p_ * inv64[d % 32]).astype(np.float32)
        sign = -1.0 if (d % 64) < 32 else 1.0
        tabs[d, 0] = np.cos(ang)
        tabs[d, 1] = sign * np.sin(ang)
    for d in range(64):
        p_ = row if d < 32 else col
        ang = (p_ * inv32[d % 16]).astype(np.float32)
        sign = -1.0 if (d % 32) < 16 else 1.0
        tabs[d, 2] = np.cos(ang)
        tabs[d, 3] = sign * np.sin(ang)
    return tabs


def _cmat():
    cm = np.zeros((128, 3, 128), np.float32)
    cm[:, 0, :] = np.eye(128, dtype=np.float32)
    for m in range(128):
        k = m + 32 if (m % 64) < 32 else m - 32
        cm[k, 1, m] = 1.0
    for m in range(64):
        k = m + 16 if (m % 32) < 16 else m - 16
        cm[k, 2, m] = 1.0
    return cm


def prep_shared(inp):
    f = lambda a: np.asarray(a, dtype=np.float32)
    w_in = f(inp["w_in"])[0]
    sh = {}
    blocks = [_blk(w_in, 1024 + 128 * j, 128) for j in range(4)]
    blocks += [_blk(w_in, 3648 + 128 * j, 128) for j in range(4)]
    blocks += [_blk(w_in, 4160 + 128 * j, 128) for j in range(4)]
    kpe_pad = np.zeros((D, 128), np.float32)
    kpe_pad[:, :64] = w_in[:, 1536:1600]
    blocks += [_blk(kpe_pad, 0, 128)]
    sh["wkv"] = np.stack(blocks)
    sh["wq"] = np.stack([_blk(w_in, 128 * j, 128) for j in range(8)] +
                        [_blk(w_in, 1600 + 128 * j, 128) for j in range(16)])
    sh["wg"] = np.stack([_blk(w_in, 4672 + ab * 4096 + 128 * c, 128) for c in range(32) for ab in range(2)])
    del w_in
    w_q_b = f(inp["w_q_b"])[0]
    wqb_pad = np.zeros((1024, 16, 256), np.float32)
    wqb_pad[:, :, :192] = w_q_b.reshape(1024, 16, 192)
    wqb_pad = wqb_pad.reshape(1024, 4096)
    sh["wqb"] = np.stack([_blk(wqb_pad, 256 * h, 256) for h in range(16)])
    w_kv_b = f(inp["w_kv_b"])[0].reshape(512, 16, 2, 128)
    grp = []
    for g in range(4):
        sub = np.concatenate([w_kv_b[:, 4 * g:4 * g + 4, 0, :].reshape(512, 512),
                              w_kv_b[:, 4 * g:4 * g + 4, 1, :].reshape(512, 512)], axis=1)
        grp.append(_blk(sub, 0, 1024))
    sh["wkvb"] = np.stack(grp)
    wab = np.concatenate([f(inp["w_branch_a"])[0], f(inp["w_branch_b"])[0]], axis=0)
    sh["wab"] = np.stack([_blk(wab, 128 * c, 128) for c in range(32)])
    del wab
    w_o = f(inp["w_o"])[0]
    sh["wo"] = np.stack([_blk(w_o, 128 * c, 128) for c in range(32)])
    w_g = f(inp["w_gate"])[0]
    w_u = f(inp["w_up"])[0]
    lst = []
    for ff in range(NFF):
        lst.append(_blk(w_g, 128 * ff, 128))
        lst.append(_blk(w_u, 128 * ff, 128))
    sh["wgu"] = np.stack(lst)
    del lst
    w_d = f(inp["w_down"])[0]
    lst = []
    for (a, b) in FF_PARTS:
        for c in range(32):
            lst.append(_blk(w_d[a * 128:b * 128], 128 * c, 128, pad=WD_PAD))
    sh["wd"] = np.stack(lst)
    g = np.zeros((128, NG), np.float32)
    g[:, G_ATTN:G_ATTN + 32] = f(inp["g_attn"])[0].reshape(32, 128).T
    g[:, G_QA:G_QA + 8] = f(inp["g_q_a"])[0].reshape(8, 128).T
    g[:, G_KVA:G_KVA + 4] = f(inp["g_kv_a"])[0].reshape(4, 128).T
    g[:, G_QN] = f(inp["g_qn"])[0]
    g[:, G_KN] = f(inp["g_kn"])[0]
    g[:, G_FFN:G_FFN + 32] = f(inp["g_ffn"])[0].reshape(32, 128).T
    g[:, G_FIN:G_FIN + 32] = f(inp["g_final"]).reshape(32, 128).T
    sh["gains"] = g
    sh["cmat"] = _cmat()
    return sh


def prep_core(x, tabs_full, core):
    b, half = core // 2, core % 2
    xT = np.asarray(x[b], dtype=np.float32).T
    order = np.concatenate([np.arange((1 - half) * T, (2 - half) * T), np.arange(half * T, (half + 1) * T)])
    xo = xT[:, order].reshape(32, 128, S).transpose(1, 0, 2)
    return {"xin": np.ascontiguousarray(xo), "tabs": np.ascontiguousarray(tabs_full[:, :, order])}


def kernel(**inputs):
    sh = prep_shared(inputs)
    tabs_full = _rope_tables()
    in_maps = []
    for core in range(8):
        m = dict(sh)
        m.update(prep_core(inputs["x"], tabs_full, core))
        in_maps.append(m)
    nc = build_program()
    res = run_bass_kernel_spmd(nc, in_maps, core_ids=list(range(8)))
    out = np.empty((4, S, D), np.float32)
    for core in range(8):
        b, half = core // 2, core % 2
        yT = res.results[core]["y"]
        out[b, half * T:(half + 1) * T, :] = yT.transpose(2, 1, 0).reshape(T, D)
    return out
```

```python
import numpy as np
import concourse.bass as bass
import concourse.mybir as mybir
from concourse.bass_utils import run_bass_kernel_spmd

F32 = mybir.dt.float32
BF16 = mybir.dt.bfloat16
AF = mybir.ActivationFunctionType
ALU = mybir.AluOpType

D = 4096
S = 2048
T = 1024
NB = 512
EPS = 1e-6
DFF = 11008
NFF = 86
MLA_SCALE = 192.0 ** -0.5
GQA_SCALE = 128.0 ** -0.5
THETA = 10000.0
FF_PARTS = [(0, 22), (22, 44), (44, 65), (65, 86)]
WD_PAD = 22 * 128
ARENA_WORDS = 53200
KPE = 9

G_ATTN, G_QA, G_KVA, G_QN, G_KN, G_FFN, G_FIN = 0, 32, 40, 44, 45, 46, 78
NG = 110


class TT:
    __slots__ = ("w", "r", "x")

    def __init__(self, x=False):
        self.w = None
        self.r = {}
        self.x = x


def tts(n):
    return [TT() for _ in range(n)]


class Prog:
    ENG = ("pe", "act", "dve", "pool", "sp")

    def __init__(self, nc):
        self.nc = nc
        self.st = {}
        for e in self.ENG:
            self.st[e] = dict(ops=[], sem=None, count=0, waited={}, dsems=[], dma_i=0)
        for e in ("pe", "act", "dve", "pool"):
            self.st[e]["sem"] = nc.alloc_semaphore(name="c_" + e)
        for e, n in (("sp", 8), ("pool", 8), ("act", 4)):
            self.st[e]["dsems"] = [nc.alloc_semaphore(name="d_%s%d" % (e, i)) for i in range(n)]
        self.semkey = {}

    def _key(self, sem):
        return sem.num

    def _wait(self, e, sem, val):
        st = self.st[e]
        if e == "pe" and sem is st["sem"]:
            return
        k = self._key(sem)
        if st["waited"].get(k, 0) >= val:
            return
        st["waited"][k] = val
        st["ops"].append(("w", sem, val))

    def _deps(self, e, reads, writes):
        for t in reads:
            if t.w is not None:
                self._wait(e, t.w[0], t.w[1])
        for t in writes:
            if t.w is not None:
                self._wait(e, t.w[0], t.w[1])
            for ev in t.r.values():
                self._wait(e, ev[0], ev[1])

    def _mark(self, ev, reads, writes):
        k = self._key(ev[0])
        for t in reads:
            t.r[k] = ev
        for t in writes:
            t.w = ev
            t.r = {}

    def op(self, e, fn, reads=(), writes=()):
        st = self.st[e]
        if any(t.x for t in reads):
            writes = tuple(writes) + tuple(t for t in reads if t.x)
            reads = tuple(t for t in reads if not t.x)
        self._deps(e, reads, writes)
        st["count"] += 1
        ev = (st["sem"], st["count"])
        st["ops"].append(("o", fn))
        self._mark(ev, reads, writes)

    def dma(self, q, out, in_, reads=(), writes=()):
        st = self.st[q]
        n = len(st["dsems"])
        i = st["dma_i"]
        st["dma_i"] += 1
        sem = st["dsems"][i % n]
        val = 16 * (i // n + 1)
        if i >= n:
            self._wait(q, sem, val - 16)
        self._deps(q, reads, writes)
        st["ops"].append(("d", out, in_, sem))
        self._mark((sem, val), reads, writes)

    def all_events(self):
        evs = []
        for e in self.ENG:
            st = self.st[e]
            if st["sem"] is not None and st["count"] > 0:
                evs.append((st["sem"], st["count"]))
            n = len(st["dsems"])
            for j, sem in enumerate(st["dsems"]):
                cnt = (st["dma_i"] - j + n - 1) // n if st["dma_i"] > j else 0
                if cnt > 0:
                    evs.append((sem, 16 * cnt))
        return evs

    def barrier(self):
        evs = self.all_events()
        for e in self.ENG:
            for sem, val in evs:
                self._wait(e, sem, val)

    def finish(self):
        for sem, val in self.all_events():
            self._wait("sp", sem, val)

    def replay(self, e, eng):
        st = self.st[e]
        sem = st["sem"]
        for o in st["ops"]:
            if o[0] == "w":
                eng.wait_ge(o[1], o[2])
            elif o[0] == "o":
                o[1](eng).then_inc(sem, 1)
            else:
                eng.dma_start(out=o[1], in_=o[2]).then_inc(o[3], 16)


class Arena:
    def __init__(self, big):
        self.big = big
        self.top = 0
        self.peak = 0

    def _alloc(self, nbytes):
        off = self.top
        self.top += (nbytes + 63) // 64 * 64
        assert self.top <= getattr(self, "limit", ARENA_WORDS * 4), ("SBUF arena overflow", self.top)
        self.peak = max(self.peak, self.top)
        return off

    def _view(self, shape, dt, esz):
        n = 1
        for s_ in shape[1:]:
            n *= s_
        off = self._alloc(n * esz)
        ap = self.big[0:shape[0], off // 4: off // 4 + (n * esz) // 4]
        if dt is not F32:
            ap = ap.bitcast(dt)
        if len(shape) == 3:
            ap = ap.rearrange("p (a b) -> p a b", a=shape[1])
        elif len(shape) == 4:
            ap = ap.rearrange("p (a b c) -> p a b c", a=shape[1], b=shape[2])
        return ap

    def f32(self, shape):
        return self._view(shape, F32, 4)

    def bf16(self, shape):
        return self._view(shape, BF16, 2)

    def sub(self, off, nbytes):
        a = Arena(self.big)
        a.top = off
        a.limit = off + nbytes
        return a

    def mark(self):
        return self.top

    def release(self, m):
        self.top = m


class Pool_:
    def __init__(self, aps):
        self.aps = aps
        self.t = tts(len(aps))
        self.i = 0

    def get(self):
        j = self.i % len(self.aps)
        self.i += 1
        return self.aps[j], self.t[j]


class Ring:
    def __init__(self, P, A, nslots):
        self.P = P
        self.buf = A.bf16([128, nslots, 4096])
        self.t = tts(nslots)
        self.n = nslots
        self.i = 0

    def fetch(self, src, nelem):
        j = self.i % self.n
        self.i += 1
        self.P.dma("pool", self.buf[:, j, 0:nelem], src, reads=(), writes=(self.t[j],))
        return self.buf[:, j, 0:nelem], self.t[j]


def i_act(out, in_, func, scale=None):
    if scale is None:
        return lambda e: e.activation(out=out, in_=in_, func=func)
    return lambda e: e.activation(out=out, in_=in_, func=func, scale=scale)


def i_ts(out, in0, s1, s2, op0, op1):
    return lambda e: e.tensor_scalar(out=out, in0=in0, scalar1=s1, scalar2=s2, op0=op0, op1=op1)


def i_stt(out, in0, scalar, in1, op0=ALU.mult, op1=ALU.mult):
    return lambda e: e.scalar_tensor_tensor(out=out, in0=in0, scalar=scalar, in1=in1, op0=op0, op1=op1)


def i_tt(out, in0, in1, op):
    return lambda e: e.tensor_tensor(out=out, in0=in0, in1=in1, op=op)


def i_copy(out, in_):
    return lambda e: e.tensor_copy(out=out, in_=in_)


def i_recip(out, in_):
    return lambda e: e.reciprocal(out=out, in_=in_)


def i_mm(out, pairs, start=True, stop=True):
    def fn(t):
        n = len(pairs)
        ins = None
        for i, (l, r) in enumerate(pairs):
            ins = t.matmul(out, l, r, start=(start and i == 0), stop=(stop and i == n - 1))
        return ins
    return fn


def i_mm1(out, l, r, start, stop):
    return lambda t: t.matmul(out, l, r, start=start, stop=stop)


def emit_kouter(P, groups):
    nmax = max(len(g["pairs"]) for g in groups)
    for k in range(nmax):
        for g in groups:
            n = len(g["pairs"])
            if k < n:
                l, r = g["pairs"][k]
                P.op("pe", i_mm1(g["out"], l, r, k == 0, k == n - 1),
                     reads=tuple(g["reads"]) + tuple(g["preads"][k]), writes=(g["out_t"],))


def i_tr(outs_ins, ident):
    def fn(t):
        ins = None
        for o, i_ in outs_ins:
            ins = t.transpose(o, i_, ident)
        return ins
    return fn


def pipeline(items, stages):
    n = len(items)
    maxlag = max(l for _, l in stages)
    for s_ in range(n + maxlag):
        for fn, lag in stages:
            j = s_ - lag
            if 0 <= j < n:
                fn(items[j])


def build_program(stages=(1, 2, 3, 4, 5), dbg=False):
    nc = bass.Bass("TRN2", target_bir_lowering=False)

    def din(name, shape):
        return nc.dram_tensor(name, list(shape), F32, kind="ExternalInput").ap()

    def dscr(name, shape, dt):
        return nc.dram_tensor(name, list(shape), dt, kind="ExternalOutput" if dbg else "Internal").ap()

    xin = din("xin", [128, 32, S])
    tabs = din("tabs", [128, 4, S])
    cmat = din("cmat", [128, 3, 128])
    gains = din("gains", [128, NG])
    wkv = din("wkv", [13, 128, 4096])
    wq = din("wq", [24, 128, 4096])
    wqb = wkvb = None
    if 2 in stages:
        wqb = din("wqb", [16, 128, 2048])
        wkvb = din("wkvb", [4, 128, 4096])
    wg = wab = wo = wgu = wd = None
    if 3 in stages:
        wg = din("wg", [64, 128, 4096])
        wab = din("wab", [32, 128, 4096])
    if 4 in stages:
        wo = din("wo", [32, 128, 4096])
    if 5 in stages:
        wgu = din("wgu", [172, 128, 4096])
        wd = din("wd", [len(FF_PARTS) * 32, 128, WD_PAD])
    y = nc.dram_tensor("y", [128, 32, T], F32, kind="ExternalOutput").ap()

    ckv_s = dscr("ckv_s", [128, 4, S], BF16)
    kpe_s = dscr("kpe_s", [128, S], BF16)
    cq_s = dscr("cq_s", [128, 8, T], BF16)
    qb_s = dscr("qb_s", [128, 16, T], BF16)
    hs_s = dscr("hs_s", [128, 32, T], BF16)
    m_s = dscr("m_s", [128, 32, T], BF16)
    x2_s = dscr("x2_s", [128, 32, T], F32)
    acc_s = dscr("acc_s", [128, 32, T], F32)
    if dbg:
        kb_d = dscr("kb_d", [128, 4, S], BF16)
        vb_d = dscr("vb_d", [128, 16, 512], BF16)
        o_d = dscr("o_d", [128, 32, T], BF16)
    ckv_st = tts(1)
    kpe_st = tts(1)
    cq_st = tts(1)
    qb_st = tts(16)
    hs_st = tts(1)
    m_st = tts(32)
    x2_st = [tts(2) for _ in range(32)]
    acc_st = [tts(2) for _ in range(32)]

    big = nc.alloc_sbuf_tensor("big", [128, ARENA_WORDS], F32) if hasattr(nc, "alloc_sbuf_tensor") else None
    ctx_big = None
    if big is None:
        ctx_big = nc.sbuf_tensor("big", [128, ARENA_WORDS], F32)
        big = ctx_big.__enter__()
    ctx_ps = nc.psum_tensor("ps", [128, 8, 512], F32)
    ps = ctx_ps.__enter__()
    psT = [TT(x=True) for _ in range(8)]

    P = Prog(nc)
    A = Arena(big)

    cm = A.bf16([128, 3, 128])
    cm_t = TT()
    gn = A.f32([128, NG])
    gn_t = TT()
    ones = A.bf16([128, 128])
    ones_t = TT()
    P.dma("pool", cm, cmat, writes=(cm_t,))
    P.dma("sp", gn, gains, writes=(gn_t,))
    P.op("dve", lambda e: e.memset(ones, 1.0), writes=(ones_t,))
    ident = cm[:, 0, :]
    perm_g = cm[:, 1, :]
    perm_m = cm[:, 2, :]
    CT = (cm_t, gn_t, ones_t)

    def rstd_ops(dst, dst_t, src_ps, src_t, n, np_=128):
        P.op("dve", i_ts(dst[0:np_], src_ps, 1.0 / n, EPS, ALU.mult, ALU.add), reads=(src_t,), writes=(dst_t,))
        P.op("act", i_act(dst[0:np_], dst[0:np_], AF.Sqrt), reads=(dst_t,), writes=(dst_t,))
        P.op("dve", i_recip(dst[0:np_], dst[0:np_]), reads=(dst_t,), writes=(dst_t,))

    rstd2 = A.f32([128, 2, NB])
    rstd2_t = tts(2)
    rstd3 = A.f32([128, 2, NB])
    rstd3_t = tts(2)
    m_glob = A.mark()

    a64_off = A.mark()
    oT = A.bf16([128, 32, T])
    o_t = [tts(2) for _ in range(32)]
    A1 = A.sub(a64_off, 65536)
    kbT = A.bf16([128, 4, S])
    kbT_t = tts(4)
    vb = A.bf16([128, 16, 512])
    vb_t = tts(16)
    m_kv = A.mark()

    if 1 in stages:
        ring = Ring(P, A, 6)
        hT = A1.bf16([128, 32, NB])
        hT_t = tts(32)
        NXS = 3
        xst = [A.f32([128, 2, NB]) for _ in range(NXS)]
        xst_t = tts(NXS)
        sqx = [A.bf16([128, 2, NB]) for _ in range(NXS)]
        sqx_t = tts(NXS)
        rx = A.f32([128, NB])
        rx_t = TT()
        tab = A1.f32([128, 4, NB])
        tab_t = TT()
        raw_kv = A1.f32([128, 4, NB])
        raw_kv_t = tts(4)
        raw_qa = A1.f32([128, 8, NB])
        raw_qa_t = tts(8)
        zv16 = A.bf16([128, 4, NB])
        zv16_t = tts(4)
        RAWP = Pool_([A.f32([128, NB]) for _ in range(4)])
        TF = Pool_([A.f32([128, NB]) for _ in range(4)])
        SQP = Pool_([A.bf16([128, NB]) for _ in range(4)])
        K16P = Pool_([A.bf16([128, NB]) for _ in range(4)])
        TB = Pool_([A.bf16([128, NB]) for _ in range(4)])
        PB = [0, 1, 2, 3, 4]
        SBK, SB2, SB3 = 5, 6, 7
        pbi = [0]
        ps7b = ps[:, SB3, :].bitcast(BF16)

        def p1_cg(tb_, cg):
            t0_ = tb_ * NB
            b = cg % NXS
            P.dma("sp", xst[b], xin[:, 2 * cg:2 * cg + 2, t0_:t0_ + NB], writes=(xst_t[b],))
            P.op("act", i_act(sqx[b], xst[b], AF.Square), reads=(xst_t[b],), writes=(sqx_t[b],))
            P.op("pe", i_mm(ps[:, SBK, :], [(ones, sqx[b][:, 0, :]), (ones, sqx[b][:, 1, :])],
                            start=(cg == 0), stop=(cg == 15)),
                 reads=(sqx_t[b], ones_t), writes=(psT[SBK],))

        def p2(tb_):
            t0_ = tb_ * NB
            o0_ = (tb_ - 2) * NB
            rstd_ops(rx, rx_t, ps[:, SBK, :], psT[SBK], D)
            for cg in range(16):
                b = cg % NXS
                P.dma("sp", xst[b], xin[:, 2 * cg:2 * cg + 2, t0_:t0_ + NB], writes=(xst_t[b],))
                for i in range(2):
                    c = 2 * cg + i
                    P.op("dve", i_stt(hT[:, c, :], xst[b][:, i, :], gn[:, G_ATTN + c:G_ATTN + c + 1], rx),
                         reads=(xst_t[b], rx_t, gn_t), writes=(hT_t[c],))
            if tb_ >= 2:
                for c0 in range(0, 32, 8):
                    P.dma("sp", hs_s[:, c0:c0 + 8, o0_:o0_ + NB], hT[:, c0:c0 + 8, :],
                          reads=tuple(hT_t[c0:c0 + 8]), writes=(hs_st[0],))

        pend_p1 = []
        for cg in range(16):
            p1_cg(0, cg)
        for tb in range(4):
            own = tb >= 2
            t0 = tb * NB
            o0 = (tb - 2) * NB
            p2(tb)
            P.dma("sp", tab, tabs[:, :, t0:t0 + NB], writes=(tab_t,))
            if tb + 1 < 4:
                pend_p1 = [(tb + 1, cg) for cg in range(16)]

            items = []
            for j in range(4):
                items.append(("kva", j, wkv[j], 128))
            for j in range(4):
                items.append(("k", j, wkv[4 + j], 128))
            for j in range(4):
                items.append(("v", j, wkv[8 + j], 128))
            items.append(("kpe", 0, wkv[12], 128))
            if own:
                for j in range(8):
                    items.append(("qa", j, wq[j], 128))
                for j in range(16):
                    items.append(("q", j, wq[8 + j], 128))
            items = [dict(kind=k, j=j, src=src, nb=nb) for (k, j, src, nb) in items]

            def st_proj(it):
                if it.get("done"):
                    return
                nb = it["nb"]
                sl, sl_t = ring.fetch(it["src"][:, 0:32 * nb], 32 * nb)
                w = sl.rearrange("p (c n) -> p c n", c=32)
                bank = PB[pbi[0] % len(PB)]
                pbi[0] += 1
                it["bank"] = bank
                P.op("pe", i_mm(ps[0:nb, bank, :], [(w[:, kc, :], hT[:, kc, :]) for kc in range(32)]),
                     reads=(sl_t,) + tuple(hT_t), writes=(psT[bank],))
                for _ in range(2):
                    if pend_p1:
                        p1_cg(*pend_p1.pop(0))

            def st_a(it, tb=tb, t0=t0, o0=o0):
                k, j, bank = it["kind"], it["j"], it["bank"]
                src = ps[:, bank, :]
                bt = psT[bank]
                if k == "kva":
                    P.op("act", i_act(raw_kv[:, j, :], src, AF.Copy), reads=(bt,), writes=(raw_kv_t[j],))
                    sq, sq_t = SQP.get()
                    P.op("act", i_act(sq, src, AF.Square), reads=(bt,), writes=(sq_t,))
                    it["sq"] = (sq, sq_t)
                elif k == "qa":
                    P.op("act", i_act(raw_qa[:, j, :], src, AF.Copy), reads=(bt,), writes=(raw_qa_t[j],))
                    sq, sq_t = SQP.get()
                    P.op("act", i_act(sq, src, AF.Square), reads=(bt,), writes=(sq_t,))
                    it["sq"] = (sq, sq_t)
                elif k in ("k", "q"):
                    raw, raw_t = RAWP.get()
                    P.op("act", i_act(raw, src, AF.Copy), reads=(bt,), writes=(raw_t,))
                    sq, sq_t = SQP.get()
                    P.op("act", i_act(sq, src, AF.Square), reads=(bt,), writes=(sq_t,))
                    it["raw"] = (raw, raw_t)
                    it["sq"] = (sq, sq_t)
                elif k == "v":
                    P.op("act", i_act(zv16[:, j, :], src, AF.Copy), reads=(bt,), writes=(zv16_t[j],))
                elif k == "kpe" and KPE >= 1:
                    raw, raw_t = RAWP.get()
                    P.op("act", i_act(raw, ps[:, bank, :], AF.Copy), reads=(bt,), writes=(raw_t,))
                    k16, k16_t = K16P.get()
                    P.op("dve", i_copy(k16, ps[:, bank, :]), reads=(bt,), writes=(k16_t,))
                    it["raw"] = (raw, raw_t)
                    it["k16"] = (k16, k16_t)

            def st_b(it):
                k, j = it["kind"], it["j"]
                if k in ("kva", "qa"):
                    last = 3 if k == "kva" else 7
                    sq, sq_t = it["sq"]
                    P.op("pe", i_mm(ps[:, SB2, :], [(ones, sq)], start=(j == 0), stop=(j == last)),
                         reads=(sq_t, ones_t), writes=(psT[SB2],))
                elif k in ("k", "q"):
                    sq, sq_t = it["sq"]
                    P.op("pe", i_mm(ps[:, SB2, :], [(ones, sq)]), reads=(sq_t, ones_t), writes=(psT[SB2],))
                elif k == "v" and j == 3:
                    for tc in range(4):
                        P.op("pe", i_tr([(ps7b[:, jj * 128:(jj + 1) * 128], zv16[:, jj, tc * 128:(tc + 1) * 128])
                                         for jj in range(4)], ident),
                             reads=tuple(zv16_t) + (cm_t,), writes=(psT[SB3],))
                        kc = tb_cur[0] * 4 + tc
                        P.op("dve", i_copy(vb[:, kc, :], ps7b[:, 0:512]), reads=(psT[SB3],), writes=(vb_t[kc],))

            def st_c(it, t0=t0, o0=o0):
                k, j = it["kind"], it["j"]
                if k == "kva" and j == 3:
                    rs, rs_t = TF.get()
                    rstd_ops(rs, rs_t, ps[:, SB2, :], psT[SB2], 512)
                    for jj in range(4):
                        o16, o16_t = TB.get()
                        P.op("dve", i_stt(o16, raw_kv[:, jj, :], gn[:, G_KVA + jj:G_KVA + jj + 1], rs),
                             reads=(raw_kv_t[jj], rs_t, gn_t), writes=(o16_t,))
                        P.dma("sp", ckv_s[:, jj, t0:t0 + NB], o16, reads=(o16_t,), writes=(ckv_st[0],))
                elif k == "qa" and j == 7:
                    rs, rs_t = TF.get()
                    rstd_ops(rs, rs_t, ps[:, SB2, :], psT[SB2], 1024)
                    for jj in range(8):
                        o16, o16_t = TB.get()
                        P.op("dve", i_stt(o16, raw_qa[:, jj, :], gn[:, G_QA + jj:G_QA + jj + 1], rs),
                             reads=(raw_qa_t[jj], rs_t, gn_t), writes=(o16_t,))
                        P.dma("sp", cq_s[:, jj, o0:o0 + NB], o16, reads=(o16_t,), writes=(cq_st[0],))
                elif k in ("k", "q"):
                    rs, rs_t = TF.get()
                    rstd_ops(rs, rs_t, ps[:, SB2, :], psT[SB2], 128)
                    raw, raw_t = it["raw"]
                    gcol = G_KN if k == "k" else G_QN
                    P.op("dve", i_stt(raw, raw, gn[:, gcol:gcol + 1], rs), reads=(raw_t, rs_t, gn_t), writes=(raw_t,))
                    k16, k16_t = K16P.get()
                    P.op("act", i_act(k16, raw, AF.Copy), reads=(raw_t,), writes=(k16_t,))
                    it["k16"] = (k16, k16_t)

            def st_d(it):
                k = it["kind"]
                if k in ("k", "q"):
                    k16, k16_t = it["k16"]
                    P.op("pe", i_mm(ps[:, SB3, :], [(perm_g, k16)]), reads=(k16_t, cm_t), writes=(psT[SB3],))
                elif k == "kpe" and KPE >= 2:
                    k16, k16_t = it["k16"]
                    P.op("pe", i_mm(ps[:, SB3, :], [(perm_m, k16)]), reads=(k16_t, cm_t), writes=(psT[SB3],))

            def st_e(it, t0=t0, o0=o0):
                k, j = it["kind"], it["j"]
                if k in ("k", "q", "kpe") and (k != "kpe" or KPE >= 3):
                    np_ = 128
                    ci, si = (2, 3) if k == "kpe" else (0, 1)
                    raw, raw_t = it["raw"]
                    t2, t2_t = TF.get()
                    P.op("dve", i_tt(t2[0:np_], ps[0:np_, SB3, :], tab[0:np_, si, :], ALU.mult),
                         reads=(psT[SB3], tab_t), writes=(t2_t,))
                    P.op("dve", i_tt(raw[0:np_], raw[0:np_], tab[0:np_, ci, :], ALU.mult),
                         reads=(raw_t, tab_t), writes=(raw_t,))
                    if k == "k":
                        P.op("dve", i_tt(kbT[:, j, t0:t0 + NB], raw, t2, ALU.add),
                             reads=(raw_t, t2_t), writes=(kbT_t[tb_cur[0]],))
                    else:
                        o16, o16_t = TB.get()
                        P.op("dve", i_tt(o16[0:np_], raw[0:np_], t2[0:np_], ALU.add),
                             reads=(raw_t, t2_t), writes=(o16_t,))
                        if k == "q":
                            P.dma("sp", qb_s[:, j, o0:o0 + NB], o16, reads=(o16_t,), writes=(qb_st[j],))
                        else:
                            if KPE >= 4:
                                P.dma("sp", kpe_s[:, t0:t0 + NB], o16, reads=(o16_t,), writes=(kpe_st[0],))

            tb_cur = [tb]
            groups = []
            for it in items[:5]:
                nb_ = it["nb"]
                sl, sl_t = ring.fetch(it["src"][:, 0:32 * nb_], 32 * nb_)
                w_ = sl.rearrange("p (c n) -> p c n", c=32)
                bank = PB[pbi[0] % len(PB)]
                pbi[0] += 1
                it["bank"] = bank
                it["done"] = True
                groups.append(dict(out=ps[0:nb_, bank, :], out_t=psT[bank],
                                   pairs=[(w_[:, kc, :], hT[:, kc, :]) for kc in range(32)],
                                   preads=[(hT_t[kc],) for kc in range(32)], reads=(sl_t,)))
            emit_kouter(P, groups)
            pipeline(items, [(st_proj, 0), (st_a, 0), (st_b, 1), (st_c, 1), (st_d, 2), (st_e, 2)])
            while pend_p1:
                p1_cg(*pend_p1.pop(0))

        if dbg:
            for tbb in range(4):
                P.dma("sp", kb_d[:, :, tbb * NB:(tbb + 1) * NB], kbT[:, :, tbb * NB:(tbb + 1) * NB],
                      reads=(kbT_t[tbb],))
            P.dma("sp", vb_d, vb, reads=tuple(vb_t))
        P.barrier()
    A.release(m_kv)


    if 2 in stages:
        def attn_phase(tasks, PT, PT_t, rec, rec_t):
            steps = []
            for ti, tk in enumerate(tasks):
                for j in range(8):
                    steps.append((ti, j))
            n = len(steps)
            ptc = [0]
            info = {}

            def qk(si):
                ti, j = steps[si]
                tk = tasks[ti]
                if j == 0 and tk.get("pre") is not None:
                    tk["pre"]()
                sb = si % 2
                for i in range(2):
                    kc = 2 * j + i
                    P.op("pe", i_mm(ps[:, 2 * sb + i, :], tk["qk"](kc)), reads=tk["rq"], writes=(psT[2 * sb + i],))
                pt, pt_t = PT[si % 3], PT_t[si % 3]
                P.op("act", i_act(pt, ps[:, 2 * sb:2 * sb + 2, :], AF.Exp, scale=tk["scale"]),
                     reads=(psT[2 * sb], psT[2 * sb + 1]), writes=(pt_t,))

            def pv(si):
                ti, j = steps[si]
                tk = tasks[ti]
                ob, db = 4 + ti % 2, 6 + ti % 2
                pt, pt_t = PT[si % 3], PT_t[si % 3]
                mm_o = []
                for i in range(2):
                    kc = 2 * j + i
                    mm_o.append((tk["v"](kc), pt[:, i, :]))

                def fn(t, mm_o=mm_o, j=j, ob=ob, db=db, pt=pt):
                    ins = None
                    for i in range(2):
                        kc = 2 * j + i
                        t.matmul(ps[:, ob, :], mm_o[i][0], mm_o[i][1], start=(kc == 0), stop=(kc == 15))
                        ins = t.matmul(ps[:, db, :], ones, pt[:, i, :], start=(kc == 0), stop=(kc == 15))
                    return ins
                P.op("pe", fn, reads=(pt_t, ones_t) + tk["rv"], writes=(psT[ob], psT[db]))
                if j == 7:
                    r, r_t = rec[ti % 2], rec_t[ti % 2]
                    P.op("dve", i_recip(r, ps[:, db, :]), reads=(psT[db],), writes=(r_t,))
                    P.op("dve", i_tt(tk["dst"], ps[:, ob, :], r, ALU.mult), reads=(psT[ob], r_t), writes=(tk["dst_t"],))
                    if tk.get("post") is not None:
                        tk["post"]()

            pending = None
            for si in range(n):
                ti, j = steps[si]
                if j == 0 and tasks[ti].get("flush") and pending is not None:
                    pv(pending)
                    pending = None
                qk(si)
                if pending is not None:
                    pv(pending)
                pending = si
            pv(pending)

        m2 = A.mark()
        PT = [A.bf16([128, 2, NB]) for _ in range(3)]
        PT_t = tts(3)
        rec = [A.f32([128, NB]) for _ in range(2)]
        rec_t = tts(2)
        qh = [A.bf16([128, T]) for _ in range(2)]
        qh_t = tts(2)
        tasks = []
        for h in range(16):
            g = h // 4
            qbuf, qbuf_t = qh[h % 2], qh_t[h % 2]
            for qb in range(2):
                def pre(h=h, qbuf=qbuf, qbuf_t=qbuf_t):
                    P.dma("sp", qbuf, qb_s[:, h, :], reads=(qb_st[h],), writes=(qbuf_t,))
                tasks.append(dict(
                    qk=(lambda kc, g=g, qbuf=qbuf, qb=qb: [(kbT[:, g, kc * 128:(kc + 1) * 128], qbuf[:, qb * NB:(qb + 1) * NB])]),
                    v=(lambda kc, g=g: vb[:, kc, g * 128:(g + 1) * 128]),
                    rq=tuple(kbT_t) + (qbuf_t,), rv=tuple(vb_t),
                    dst=oT[:, 16 + h, qb * NB:(qb + 1) * NB], dst_t=o_t[16 + h][qb], scale=GQA_SCALE,
                    pre=(pre if qb == 0 else None)))
        attn_phase(tasks, PT, PT_t, rec, rec_t)
        P.barrier()
        A.release(a64_off + 65536)

        ring = Ring(P, A, 3)
        PT = [A.bf16([128, 2, NB]) for _ in range(3)]
        PT_t = tts(3)
        rec = [A.f32([128, NB]) for _ in range(2)]
        rec_t = tts(2)
        ckvT = A.bf16([128, 4, S])
        ckvT_t = TT()
        kpeT = A.bf16([128, S])
        kpeT_t = TT()
        cqT = A.bf16([128, 8, T])
        cqT_t = TT()
        tabm = A.f32([128, 2, T])
        tabm_t = TT()
        knope = A.bf16([128, 4, S])
        knope_t = tts(4)
        va = A.bf16([128, 16, 512])
        va_t = tts(16)
        qn = [A.bf16([128, T]) for _ in range(2)]
        qn_t = tts(2)
        qp = [A.bf16([128, T]) for _ in range(2)]
        qp_t = tts(2)
        RAW2 = Pool_([A.f32([128, NB]) for _ in range(2)])
        K162 = Pool_([A.bf16([128, NB]) for _ in range(2)])
        T22 = Pool_([A.f32([128, NB]) for _ in range(2)])
        P.dma("sp", ckvT, ckv_s, reads=(ckv_st[0],), writes=(ckvT_t,))
        P.dma("sp", kpeT, kpe_s, reads=(kpe_st[0],), writes=(kpeT_t,))
        P.dma("sp", cqT, cq_s, reads=(cq_st[0],), writes=(cqT_t,))
        P.dma("sp", tabm, tabs[:, 2:4, T:S], writes=(tabm_t,))
        xb = [0]

        def xbank():
            b = xb[0] % 4
            xb[0] += 1
            return b

        def expand_group(hg):
            sl, sl_t = ring.fetch(wkvb[hg], 4096)
            w = sl.rearrange("p (c n) -> p c n", c=4)
            for hl in range(4):
                for tbk in range(4):
                    b = xbank()
                    P.op("pe", i_mm(ps[:, b, :], [(w[:, kc, hl * 128:(hl + 1) * 128], ckvT[:, kc, tbk * NB:(tbk + 1) * NB])
                                                  for kc in range(4)]),
                         reads=(sl_t, ckvT_t), writes=(psT[b],))
                    eng = "act" if (hl * 4 + tbk) % 2 == 0 else "dve"
                    fn = i_act(knope[:, hl, tbk * NB:(tbk + 1) * NB], ps[:, b, :], AF.Copy) if eng == "act" else \
                        i_copy(knope[:, hl, tbk * NB:(tbk + 1) * NB], ps[:, b, :])
                    P.op(eng, fn, reads=(psT[b],), writes=(knope_t[hl],))
            for kc16 in range(16):
                b = xbank()
                P.op("pe", i_mm(ps[:, b, :], [(ckvT[:, kc, kc16 * 128:(kc16 + 1) * 128], w[:, kc, 512:1024])
                                              for kc in range(4)]),
                     reads=(sl_t, ckvT_t), writes=(psT[b],))
                eng = "act" if kc16 % 2 == 0 else "dve"
                fn = i_act(va[:, kc16, :], ps[:, b, :], AF.Copy) if eng == "act" else i_copy(va[:, kc16, :], ps[:, b, :])
                P.op(eng, fn, reads=(psT[b],), writes=(va_t[kc16],))

        prep_state = {}

        def prep_a(h):
            sl, sl_t = ring.fetch(wqb[h], 2048)
            w = sl.rearrange("p (c n) -> p c n", c=8)
            bq = h % 2
            st_ = []
            for qb in range(2):
                qs = slice(qb * NB, (qb + 1) * NB)
                b = xbank()
                P.op("pe", i_mm(ps[:, b, :], [(w[:, kc, 0:128], cqT[:, kc, qs]) for kc in range(8)]),
                     reads=(sl_t, cqT_t), writes=(psT[b],))
                P.op("dve", i_copy(qn[bq][:, qs], ps[:, b, :]), reads=(psT[b],), writes=(qn_t[bq],))
                b = xbank()
                P.op("pe", i_mm(ps[:, b, :], [(w[:, kc, 128:256], cqT[:, kc, qs]) for kc in range(8)]),
                     reads=(sl_t, cqT_t), writes=(psT[b],))
                raw, raw_t = RAW2.get()
                k16, k16_t = K162.get()
                P.op("act", i_act(raw, ps[:, b, :], AF.Copy), reads=(psT[b],), writes=(raw_t,))
                P.op("dve", i_copy(k16, ps[:, b, :]), reads=(psT[b],), writes=(k16_t,))
                st_.append((raw, raw_t, k16, k16_t, qs))
            prep_state[h] = st_

        def prep_b(h):
            bq = h % 2
            for (raw, raw_t, k16, k16_t, qs) in prep_state.pop(h):
                b = xbank()
                P.op("pe", i_mm(ps[:, b, :], [(perm_m, k16)]), reads=(k16_t, cm_t), writes=(psT[b],))
                t2, t2_t = T22.get()
                P.op("dve", i_tt(t2, ps[:, b, :], tabm[:, 1, qs], ALU.mult), reads=(psT[b], tabm_t), writes=(t2_t,))
                P.op("dve", i_tt(raw, raw, tabm[:, 0, qs], ALU.mult), reads=(raw_t, tabm_t), writes=(raw_t,))
                P.op("dve", i_tt(qp[bq][:, qs], raw, t2, ALU.add), reads=(raw_t, t2_t), writes=(qp_t[bq],))

        tasks = []
        for h in range(16):
            hg, hl = h // 4, h % 4
            bq = h % 2
            for qb in range(2):
                qs = slice(qb * NB, (qb + 1) * NB)

                def pre(h=h, hl=hl, hg=hg, qb=qb):
                    if hl == 0 and qb == 0:
                        expand_group(hg)
                    if h == 0 and qb == 0:
                        prep_a(0)
                        prep_b(0)
                    if qb == 0 and h + 1 < 16:
                        prep_a(h + 1)
                    if qb == 1 and h + 1 < 16:
                        prep_b(h + 1)

                tasks.append(dict(
                    qk=(lambda kc, hl=hl, bq=bq, qs=qs: [(knope[:, hl, kc * 128:(kc + 1) * 128], qn[bq][:, qs]),
                                                           (kpeT[:, kc * 128:(kc + 1) * 128], qp[bq][:, qs])]),
                    v=(lambda kc, hl=hl: va[:, kc, hl * 128:(hl + 1) * 128]),
                    rq=(knope_t[hl], kpeT_t, qn_t[bq], qp_t[bq]), rv=tuple(va_t),
                    dst=oT[:, h, qs], dst_t=o_t[h][qb], scale=MLA_SCALE, pre=pre, flush=(hl == 0 and qb == 0)))
        attn_phase(tasks, PT, PT_t, rec, rec_t)
        if dbg:
            P.dma("sp", o_d, oT, reads=tuple(t for pr in o_t for t in pr))
        P.barrier()
        A.release(a64_off + 65536)


    def qsl(qb):
        return slice(qb * NB, (qb + 1) * NB)

    if 3 in stages:
        ring = Ring(P, A, 6)
        hB = A.bf16([128, 32, T])
        hB_t = tts(32)
        for c_ in range(32):
            P.dma("sp", hB[:, c_, :], hs_s[:, c_, :], reads=(hs_st[0],), writes=(hB_t[c_],))
        SAp = Pool_([A.f32([128, NB]) for _ in range(2)])
        SBp = Pool_([A.f32([128, NB]) for _ in range(2)])
        TMp = Pool_([A.f32([128, NB]) for _ in range(2)])
        MSp = Pool_([A.bf16([128, NB]) for _ in range(3)])
        o_all = tuple(t for pr in o_t for t in pr)
        it3 = 0
        for c in range(32):
            slA, slA_t = ring.fetch(wg[2 * c], 4096)
            slB, slB_t = ring.fetch(wg[2 * c + 1], 4096)
            slW, slW_t = ring.fetch(wab[c], 4096)
            wA = slA.rearrange("p (c n) -> p c n", c=32)
            wB = slB.rearrange("p (c n) -> p c n", c=32)
            wW = slW.rearrange("p (c n) -> p c n", c=32)

            def mm_o(qb, b0, wW=wW, slW_t=slW_t):
                qs = qsl(qb)
                P.op("pe", i_mm(ps[:, b0 + 2, :], [(wW[:, kc, :], oT[:, kc, qs]) for kc in range(16)]),
                     reads=(slW_t,) + o_all, writes=(psT[b0 + 2],))
                P.op("pe", i_mm(ps[:, b0 + 3, :], [(wW[:, 16 + kc, :], oT[:, 16 + kc, qs]) for kc in range(16)]),
                     reads=(slW_t,) + o_all, writes=(psT[b0 + 3],))

            def g_groups(qb, b0, wA=wA, wB=wB, slA_t=slA_t, slB_t=slB_t):
                qs = qsl(qb)
                return [dict(out=ps[:, b0, :], out_t=psT[b0], pairs=[(wA[:, kc, :], hB[:, kc, qs]) for kc in range(32)],
                             preads=[(hB_t[kc],) for kc in range(32)], reads=(slA_t,)),
                        dict(out=ps[:, b0 + 1, :], out_t=psT[b0 + 1], pairs=[(wB[:, kc, :], hB[:, kc, qs]) for kc in range(32)],
                             preads=[(hB_t[kc],) for kc in range(32)], reads=(slB_t,))]

            def mm_g(qb, b0, wA=wA, wB=wB, slA_t=slA_t, slB_t=slB_t):
                qs = qsl(qb)
                P.op("pe", i_mm(ps[:, b0, :], [(wA[:, kc, :], hB[:, kc, qs]) for kc in range(32)]),
                     reads=(slA_t,) + tuple(hB_t), writes=(psT[b0],))
                P.op("pe", i_mm(ps[:, b0 + 1, :], [(wB[:, kc, :], hB[:, kc, qs]) for kc in range(32)]),
                     reads=(slB_t,) + tuple(hB_t), writes=(psT[b0 + 1],))

            def evac(qb, b0, c=c):
                qs = qsl(qb)
                sa, sa_t = SAp.get()
                sb_, sb_t = SBp.get()
                tm, tm_t = TMp.get()
                ms, ms_t = MSp.get()
                P.op("act", i_act(sa, ps[:, b0, :], AF.Sigmoid), reads=(psT[b0],), writes=(sa_t,))
                P.op("act", i_act(sb_, ps[:, b0 + 1, :], AF.Sigmoid), reads=(psT[b0 + 1],), writes=(sb_t,))
                P.op("dve", i_tt(tm, ps[:, b0 + 2, :], sa, ALU.mult), reads=(psT[b0 + 2], sa_t), writes=(tm_t,))
                P.op("dve", i_tt(sb_, ps[:, b0 + 3, :], sb_, ALU.mult), reads=(psT[b0 + 3], sb_t), writes=(sb_t,))
                P.op("dve", i_tt(ms, tm, sb_, ALU.add), reads=(tm_t, sb_t), writes=(ms_t,))
                P.dma("sp", m_s[:, c, qs], ms, reads=(ms_t,), writes=(m_st[c],))

            if c == 0:
                mm_o(0, 0)
                mm_o(1, 4)
                emit_kouter(P, g_groups(0, 0) + g_groups(1, 4))
                evac(0, 0)
                evac(1, 4)
                it3 = 2
            else:
                for qb in range(2):
                    b0 = 4 * (it3 % 2)
                    it3 += 1
                    mm_g(qb, b0)
                    mm_o(qb, b0)
                    evac(qb, b0)
        P.barrier()
    A.release(m_glob)

    def stats_tail(ps_banks, dst, dst_t):
        for qb in range(2):
            rstd_ops(dst[:, qb, :], dst_t[qb], ps[:, ps_banks[qb], :], psT[ps_banks[qb]], D)

    if 4 in stages:
        ring = Ring(P, A, 6)
        mB = A.bf16([128, 32, T])
        mB_t = tts(32)
        for c in range(32):
            P.dma("sp", mB[:, c, :], m_s[:, c, :], reads=(m_st[c],), writes=(mB_t[c],))
        XR = Pool_([A.f32([128, T]) for _ in range(2)])
        X2 = Pool_([A.f32([128, NB]) for _ in range(4)])
        SQ = Pool_([A.bf16([128, NB]) for _ in range(4)])
        pend = []
        bi = 0

        def flush_pend(lst):
            for (qb, cc, sq, sq_t) in lst:
                P.op("pe", i_mm(ps[:, 6 + qb, :], [(ones, sq)], start=(cc == 0), stop=(cc == 31)),
                     reads=(sq_t, ones_t), writes=(psT[6 + qb],))

        def s4_evac(c, qb, b, xr, xr_t):
            qs = qsl(qb)
            x2, x2_t = X2.get()
            P.op("dve", i_tt(x2, ps[:, b, :], xr[:, qs], ALU.add), reads=(psT[b], xr_t), writes=(x2_t,))
            P.dma("sp", x2_s[:, c, qs], x2, reads=(x2_t,), writes=(x2_st[c][qb],))
            sq, sq_t = SQ.get()
            P.op("act", i_act(sq, x2, AF.Square), reads=(x2_t,), writes=(sq_t,))
            return (qb, c, sq, sq_t)

        groups, firsts = [], []
        for c in range(3):
            sl, sl_t = ring.fetch(wo[c], 4096)
            w = sl.rearrange("p (c n) -> p c n", c=32)
            for qb in range(2):
                b = bi % 6
                bi += 1
                groups.append(dict(out=ps[:, b, :], out_t=psT[b], pairs=[(w[:, kc, :], mB[:, kc, qsl(qb)]) for kc in range(32)],
                                   preads=[(mB_t[kc],) for kc in range(32)], reads=(sl_t,)))
                firsts.append((c, qb, b))
        emit_kouter(P, groups)
        cur_xr = {}
        for (c, qb, b) in firsts:
            if qb == 0:
                xr, xr_t = XR.get()
                P.dma("sp", xr, xin[:, c, T:S], writes=(xr_t,))
                cur_xr[c] = (xr, xr_t)
            xr, xr_t = cur_xr[c]
            newp = [s4_evac(c, qb, b, xr, xr_t)]
            flush_pend(pend)
            pend = newp
        for c in range(3, 32):
            sl, sl_t = ring.fetch(wo[c], 4096)
            w = sl.rearrange("p (c n) -> p c n", c=32)
            xr, xr_t = XR.get()
            P.dma("sp", xr, xin[:, c, T:S], writes=(xr_t,))
            newp = []
            for qb in range(2):
                qs = qsl(qb)
                b = bi % 6
                bi += 1
                P.op("pe", i_mm(ps[:, b, :], [(w[:, kc, :], mB[:, kc, qs]) for kc in range(32)]),
                     reads=(sl_t,) + tuple(mB_t), writes=(psT[b],))
                newp.append(s4_evac(c, qb, b, xr, xr_t))
            flush_pend(pend)
            pend = newp
        flush_pend(pend)
        stats_tail((6, 7), rstd2, rstd2_t)
        P.barrier()
    A.release(m_glob)

    if 5 in stages:
        ring = Ring(P, A, 5)
        h2B = A.bf16([128, 32, T])
        h2_t = tts(32)
        actT = A.bf16([128, 22, T])
        act_t = [tts(2) for _ in range(22)]
        XL = Pool_([A.f32([128, T]) for _ in range(4)])
        SL = Pool_([A.f32([128, NB]) for _ in range(2)])
        OS = Pool_([A.f32([128, NB]) for _ in range(4)])
        SQ = Pool_([A.bf16([128, NB]) for _ in range(4)])
        for c in range(32):
            xl, xl_t = XL.get()
            P.dma("sp", xl, x2_s[:, c, :], reads=tuple(x2_st[c]), writes=(xl_t,))
            P.op("dve", i_stt(h2B[:, c, :], xl, gn[:, G_FFN + c:G_FFN + c + 1], rstd2.rearrange("p a b -> p (a b)")),
                 reads=(xl_t, rstd2_t[0], rstd2_t[1], gn_t), writes=(h2_t[c],))
        bi = 0
        for pi, (fa, fb) in enumerate(FF_PARTS):
            nk = fb - fa
            last = pi == len(FF_PARTS) - 1
            def gu_evac(fl, qb, bg):
                qs = qsl(qb)
                sl_, sl_t = SL.get()
                P.op("act", i_act(sl_, ps[:, bg, :], AF.Silu), reads=(psT[bg],), writes=(sl_t,))
                P.op("dve", i_tt(actT[:, fl, qs], ps[:, bg + 1, :], sl_, ALU.mult),
                     reads=(psT[bg + 1], sl_t), writes=(act_t[fl][qb],))

            kfirst = []
            kgroups = []
            for fl in range(nk):
                f = fa + fl
                slg, slg_t = ring.fetch(wgu[2 * f], 4096)
                slu, slu_t = ring.fetch(wgu[2 * f + 1], 4096)
                wg_ = slg.rearrange("p (c n) -> p c n", c=32)
                wu_ = slu.rearrange("p (c n) -> p c n", c=32)
                for qb in range(2):
                    qs = qsl(qb)
                    bg = 2 * (bi % 3)
                    bi += 1
                    if pi == 0 and 2 * fl + qb < 3:
                        for (bb, ww, tt_) in ((bg, wg_, slg_t), (bg + 1, wu_, slu_t)):
                            kgroups.append(dict(out=ps[:, bb, :], out_t=psT[bb],
                                                pairs=[(ww[:, kc, :], h2B[:, kc, qs]) for kc in range(32)],
                                                preads=[(h2_t[kc],) for kc in range(32)], reads=(tt_,)))
                        kfirst.append((fl, qb, bg))
                        if len(kfirst) == 3:
                            emit_kouter(P, kgroups)
                            for (fl_, qb_, bg_) in kfirst:
                                gu_evac(fl_, qb_, bg_)
                        continue
                    P.op("pe", i_mm(ps[:, bg, :], [(wg_[:, kc, :], h2B[:, kc, qs]) for kc in range(32)]),
                         reads=(slg_t,) + tuple(h2_t), writes=(psT[bg],))
                    P.op("pe", i_mm(ps[:, bg + 1, :], [(wu_[:, kc, :], h2B[:, kc, qs]) for kc in range(32)]),
                         reads=(slu_t,) + tuple(h2_t), writes=(psT[bg + 1],))
                    gu_evac(fl, qb, bg)
            pend = []
            for c in range(32):
                sld, sld_t = ring.fetch(wd[pi * 32 + c][:, 0:nk * 128], nk * 128)
                w = sld.rearrange("p (c n) -> p c n", c=nk)
                xl, xl_t = XL.get()
                if pi == 0:
                    P.dma("sp", xl, x2_s[:, c, :], reads=tuple(x2_st[c]), writes=(xl_t,))
                else:
                    P.dma("sp", xl, acc_s[:, c, :], reads=tuple(acc_st[c]), writes=(xl_t,))
                newp = []
                for qb in range(2):
                    qs = qsl(qb)
                    b = bi % 6
                    bi += 1
                    P.op("pe", i_mm(ps[:, b, :], [(w[:, k, :], actT[:, k, qs]) for k in range(nk)]),
                         reads=(sld_t,) + tuple(act_t[k][qb] for k in range(nk)), writes=(psT[b],))
                    os_, os_t = OS.get()
                    P.op("dve", i_tt(os_, ps[:, b, :], xl[:, qs], ALU.add), reads=(psT[b], xl_t), writes=(os_t,))
                    P.dma("sp", acc_s[:, c, qs], os_, reads=(os_t,), writes=(acc_st[c][qb],))
                    if last:
                        sq, sq_t = SQ.get()
                        P.op("act", i_act(sq, os_, AF.Square), reads=(os_t,), writes=(sq_t,))
                        newp.append((qb, c, sq, sq_t))
                for (qb, cc, sq, sq_t) in pend:
                    P.op("pe", i_mm(ps[:, 6 + qb, :], [(ones, sq)], start=(cc == 0), stop=(cc == 31)),
                         reads=(sq_t, ones_t), writes=(psT[6 + qb],))
                pend = newp
            for (qb, cc, sq, sq_t) in pend:
                P.op("pe", i_mm(ps[:, 6 + qb, :], [(ones, sq)], start=(cc == 0), stop=(cc == 31)),
                     reads=(sq_t, ones_t), writes=(psT[6 + qb],))
        stats_tail((6, 7), rstd3, rstd3_t)
        P.barrier()
        A.release(m_glob)
        XL = Pool_([A.f32([128, T]) for _ in range(8)])
        OS = Pool_([A.f32([128, T]) for _ in range(8)])
        r3 = rstd3.rearrange("p a b -> p (a b)")
        for c in range(32):
            xl, xl_t = XL.get()
            P.dma("sp", xl, acc_s[:, c, :], reads=tuple(acc_st[c]), writes=(xl_t,))
            os_, os_t = OS.get()
            P.op("dve", i_stt(os_, xl, gn[:, G_FIN + c:G_FIN + c + 1], r3),
                 reads=(xl_t, rstd3_t[0], rstd3_t[1], gn_t), writes=(os_t,))
            P.dma("act", y[:, c, :], os_, reads=(os_t,))
    P.finish()

    with nc.Block() as block:
        @block.tensor
        def _(e):
            P.replay("pe", e)

        @block.scalar
        def _(e):
            P.replay("act", e)

        @block.vector
        def _(e):
            P.replay("dve", e)

        @block.gpsimd
        def _(e):
            P.replay("pool", e)

        @block.sync
        def _(e):
            P.replay("sp", e)

    print("SBUF peak bytes/partition:", A.peak, "ops:", {e: len(P.st[e]["ops"]) for e in P.ENG})
    return nc


def _blk(W, c0, nb, pad=None):
    K = W.shape[0]
    kc = K // 128
    sub = W[:, c0:c0 + nb].reshape(kc, 128, nb).transpose(1, 0, 2).reshape(128, kc * nb)
    if pad is not None and pad > kc * nb:
        out = np.zeros((128, pad), np.float32)
        out[:, :kc * nb] = sub
        return out
    return np.ascontiguousarray(sub)


def _rope_tables():
    pos = np.arange(S)
    row = (pos // 64).astype(np.float32)
    col = (pos % 64).astype(np.float32)
    tabs = np.zeros((128, 4, S), np.float32)
    inv64 = (THETA ** (-np.arange(0, 64, 2, dtype=np.float32) / 64)).astype(np.float32)
    inv32 = (THETA ** (-np.arange(0, 32, 2, dtype=np.float32) / 32)).astype(np.float32)
    for d in range(128):
        p_ = row if d < 64 else col
        ang = (p_ * inv64[d % 32]).astype(np.float32)
        sign = -1.0 if (d % 64) < 32 else 1.0
        tabs[d, 0] = np.cos(ang)
        tabs[d, 1] = sign * np.sin(ang)
    for d in range(64):
        p_ = row if d < 32 else col
        ang = (p_ * inv32[d % 16]).astype(np.float32)
        sign = -1.0 if (d % 32) < 16 else 1.0
        tabs[d, 2] = np.cos(ang)
        tabs[d, 3] = sign * np.sin(ang)
    return tabs


def _cmat():
    cm = np.zeros((128, 3, 128), np.float32)
    cm[:, 0, :] = np.eye(128, dtype=np.float32)
    for m in range(128):
        k = m + 32 if (m % 64) < 32 else m - 32
        cm[k, 1, m] = 1.0
    for m in range(64):
        k = m + 16 if (m % 32) < 16 else m - 16
        cm[k, 2, m] = 1.0
    return cm


def prep_shared(inp):
    f = lambda a: np.asarray(a, dtype=np.float32)
    w_in = f(inp["w_in"])[0]
    sh = {}
    blocks = [_blk(w_in, 1024 + 128 * j, 128) for j in range(4)]
    blocks += [_blk(w_in, 3648 + 128 * j, 128) for j in range(4)]
    blocks += [_blk(w_in, 4160 + 128 * j, 128) for j in range(4)]
    kpe_pad = np.zeros((D, 128), np.float32)
    kpe_pad[:, :64] = w_in[:, 1536:1600]
    blocks += [_blk(kpe_pad, 0, 128)]
    sh["wkv"] = np.stack(blocks)
    sh["wq"] = np.stack([_blk(w_in, 128 * j, 128) for j in range(8)] +
                        [_blk(w_in, 1600 + 128 * j, 128) for j in range(16)])
    sh["wg"] = np.stack([_blk(w_in, 4672 + ab * 4096 + 128 * c, 128) for c in range(32) for ab in range(2)])
    del w_in
    w_q_b = f(inp["w_q_b"])[0]
    wqb_pad = np.zeros((1024, 16, 256), np.float32)
    wqb_pad[:, :, :192] = w_q_b.reshape(1024, 16, 192)
    wqb_pad = wqb_pad.reshape(1024, 4096)
    sh["wqb"] = np.stack([_blk(wqb_pad, 256 * h, 256) for h in range(16)])
    w_kv_b = f(inp["w_kv_b"])[0].reshape(512, 16, 2, 128)
    grp = []
    for g in range(4):
        sub = np.concatenate([w_kv_b[:, 4 * g:4 * g + 4, 0, :].reshape(512, 512),
                              w_kv_b[:, 4 * g:4 * g + 4, 1, :].reshape(512, 512)], axis=1)
        grp.append(_blk(sub, 0, 1024))
    sh["wkvb"] = np.stack(grp)
    wab = np.concatenate([f(inp["w_branch_a"])[0], f(inp["w_branch_b"])[0]], axis=0)
    sh["wab"] = np.stack([_blk(wab, 128 * c, 128) for c in range(32)])
    del wab
    w_o = f(inp["w_o"])[0]
    sh["wo"] = np.stack([_blk(w_o, 128 * c, 128) for c in range(32)])
    w_g = f(inp["w_gate"])[0]
    w_u = f(inp["w_up"])[0]
    lst = []
    for ff in range(NFF):
        lst.append(_blk(w_g, 128 * ff, 128))
        lst.append(_blk(w_u, 128 * ff, 128))
    sh["wgu"] = np.stack(lst)
    del lst
    w_d = f(inp["w_down"])[0]
    lst = []
    for (a, b) in FF_PARTS:
        for c in range(32):
            lst.append(_blk(w_d[a * 128:b * 128], 128 * c, 128, pad=WD_PAD))
    sh["wd"] = np.stack(lst)
    g = np.zeros((128, NG), np.float32)
    g[:, G_ATTN:G_ATTN + 32] = f(inp["g_attn"])[0].reshape(32, 128).T
    g[:, G_QA:G_QA + 8] = f(inp["g_q_a"])[0].reshape(8, 128).T
    g[:, G_KVA:G_KVA + 4] = f(inp["g_kv_a"])[0].reshape(4, 128).T
    g[:, G_QN] = f(inp["g_qn"])[0]
    g[:, G_KN] = f(inp["g_kn"])[0]
    g[:, G_FFN:G_FFN + 32] = f(inp["g_ffn"])[0].reshape(32, 128).T
    g[:, G_FIN:G_FIN + 32] = f(inp["g_final"]).reshape(32, 128).T
    sh["gains"] = g
    sh["cmat"] = _cmat()
    return sh


def prep_core(x, tabs_full, core):
    b, half = core // 2, core % 2
    xT = np.asarray(x[b], dtype=np.float32).T
    order = np.concatenate([np.arange((1 - half) * T, (2 - half) * T), np.arange(half * T, (half + 1) * T)])
    xo = xT[:, order].reshape(32, 128, S).transpose(1, 0, 2)
    return {"xin": np.ascontiguousarray(xo), "tabs": np.ascontiguousarray(tabs_full[:, :, order])}


def kernel(**inputs):
    sh = prep_shared(inputs)
    tabs_full = _rope_tables()
    in_maps = []
    for core in range(8):
        m = dict(sh)
        m.update(prep_core(inputs["x"], tabs_full, core))
        in_maps.append(m)
    nc = build_program()
    res = run_bass_kernel_spmd(nc, in_maps, core_ids=list(range(8)))
    out = np.empty((4, S, D), np.float32)
    for core in range(8):
        b, half = core // 2, core % 2
        yT = res.results[core]["y"]
        out[b, half * T:(half + 1) * T, :] = yT.transpose(2, 1, 0).reshape(T, D)
    return out
```
